# Optimizing a Trainium2 kernel written in Bass

```python
import jax, jax.numpy as jnp
from jax import lax
import numpy as np

D_MODEL = 1024
BATCH = 2
SEQ = 8192
DEPTH = 2

N_BRANCH = 4
BRANCH_W = D_MODEL // N_BRANCH
N_GROUPS = 4
GROUP_W = BRANCH_W // N_GROUPS
SHORTCONV_K = 3
GLA_CHUNK = 64
GLA_RANK = 16
GLA_TAU = 16.0
SGU_CHUNK = 128
LRU_CONV_K = 4
LRU_C = 8.0
N_MEM = 256
XA_HEADS = 4
XA_HEAD_DIM = D_MODEL // XA_HEADS
D_FF = ((8 * D_MODEL // 3 + 127) // 128) * 128
FFN_CONV_K = 3
N_NORMS = 7
EPS = 1e-6
IN_WIDTHS = (BRANCH_W, BRANCH_W, BRANCH_W,
             BRANCH_W, BRANCH_W, BRANCH_W, BRANCH_W, GLA_RANK,
             BRANCH_W, BRANCH_W,
             BRANCH_W, BRANCH_W)
D_IN = sum(IN_WIDTHS)
SPLIT_POINTS = tuple(int(v) for v in np.cumsum(IN_WIDTHS)[:-1])

kernel_name = 'hybrid_gated_parallel_mixer_trunk'


def rms_norm(x, g):
    xf = x.astype(jnp.float32)
    y = xf * lax.rsqrt(jnp.mean(xf * xf, axis=-1, keepdims=True) + EPS)
    return (y * g.astype(jnp.float32)).astype(x.dtype)


def layer_norm(x, g, b):
    xf = x.astype(jnp.float32)
    mu = jnp.mean(xf, axis=-1, keepdims=True)
    var = jnp.mean(jnp.square(xf - mu), axis=-1, keepdims=True)
    y = (xf - mu) * lax.rsqrt(var + EPS) * g.astype(jnp.float32) + b.astype(jnp.float32)
    return y.astype(x.dtype)


def causal_dwconv(x, w, b=None):
    k = w.shape[0]
    y = lax.conv_general_dilated(
        x, w[:, None, :].astype(x.dtype), window_strides=(1,), padding=[(k - 1, 0)],
        dimension_numbers=('NWC', 'WIO', 'NWC'), feature_group_count=x.shape[-1])
    if b is not None:
        y = y + b.astype(x.dtype)
    return y


def shortconv_branch(bg, cg, xin, conv_w):
    return bg * causal_dwconv(cg * xin, conv_w)


def gla_branch(q, k, v, r, a_lr, w_alpha, b_alpha, norm_g):
    bsz, s, _ = q.shape
    n = s // GLA_CHUNK

    def heads(t):
        return t.astype(jnp.float32).reshape(bsz, n, GLA_CHUNK, N_GROUPS, GROUP_W)

    qh = heads(q) * (GROUP_W ** -0.5)
    kh, vh = heads(k), heads(v)
    glog = jax.nn.log_sigmoid((a_lr @ w_alpha + b_alpha).astype(jnp.float32)) / GLA_TAU
    gcum = jnp.cumsum(heads(glog), axis=2)
    g_last = gcum[:, :, -1]
    q_dec = qh * jnp.exp(gcum)
    k_dec = kh * jnp.exp(-gcum)
    causal = jnp.tril(jnp.ones((GLA_CHUNK, GLA_CHUNK), dtype=bool))
    scores = jnp.where(causal, jnp.einsum('bnihd,bnjhd->bnhij', q_dec, k_dec), 0.0)
    o_intra = jnp.einsum('bnhij,bnjhv->bnihv', scores, vh)
    k_to_end = kh * jnp.exp(g_last[:, :, None] - gcum)
    kv = jnp.einsum('bnjhd,bnjhv->bnhdv', k_to_end, vh)

    def step(state, inp):
        decay, kv_n = inp
        return jnp.exp(decay)[..., None] * state + kv_n, state

    init = jnp.zeros((bsz, N_GROUPS, GROUP_W, GROUP_W), jnp.float32)
    _, s_prev = lax.scan(step, init, (jnp.moveaxis(g_last, 1, 0), jnp.moveaxis(kv, 1, 0)))
    s_prev = jnp.moveaxis(s_prev, 0, 1)
    o = o_intra + jnp.einsum('bnihd,bnhdv->bnihv', q_dec, s_prev)
    o = o * lax.rsqrt(jnp.mean(o * o, axis=-1, keepdims=True) + EPS)
    o = o.reshape(bsz, s, BRANCH_W) * norm_g.astype(jnp.float32)
    return o.astype(q.dtype) * jax.nn.silu(r)


def sgu_branch(u, v, ln_g, ln_b, w_s, b_s):
    bsz, s, _ = v.shape
    n = s // SGU_CHUNK
    vn = layer_norm(v, ln_g, ln_b).reshape(bsz, n, SGU_CHUNK, N_GROUPS, GROUP_W)
    mask = jnp.tril(jnp.ones((SGU_CHUNK, SGU_CHUNK), dtype=w_s.dtype))
    mixed = jnp.einsum('gij,bnjgc->bnigc', w_s * mask, vn) + b_s.T[:, :, None]
    return u * mixed.reshape(bsz, s, BRANCH_W)


def rglru_branch(xr, gate, conv_w, conv_b, w_a, b_a, w_x, b_x, lam):
    bsz, s, _ = xr.shape
    xc = causal_dwconv(xr, conv_w, conv_b)
    xg = xc.reshape(bsz, s, N_GROUPS, GROUP_W)
    r = jax.nn.sigmoid(jnp.einsum('bsgi,gio->bsgo', xg, w_a).reshape(bsz, s, BRANCH_W) + b_a)
    i = jax.nn.sigmoid(jnp.einsum('bsgi,gio->bsgo', xg, w_x).reshape(bsz, s, BRANCH_W) + b_x)
    log_a = -LRU_C * r.astype(jnp.float32) * jax.nn.softplus(-lam.astype(jnp.float32))
    a = jnp.exp(log_a)
    mult = jnp.sqrt(-jnp.expm1(2.0 * log_a))
    bx = mult * (i * xc).astype(jnp.float32)

    def combine(lft, rgt):
        a1, b1 = lft
        a2, b2 = rgt
        return a1 * a2, a2 * b1 + b2

    _, h = lax.associative_scan(combine, (a, bx), axis=1)
    return h.astype(xr.dtype) * jax.nn.gelu(gate)


def hybrid_mixer(h, w_in, sc_conv_w, gla_w_alpha, gla_b_alpha, gla_norm_g,
                 sgu_ln_g, sgu_ln_b, sgu_w, sgu_b,
                 lru_conv_w, lru_conv_b, lru_w_a, lru_b_a, lru_w_x, lru_b_x, lru_lambda,
                 w_gate, b_gate, w_branch, w_mix_out):
    bsz, s, _ = h.shape
    proj = h @ w_in
    (a_b, a_c, a_x, q, k, v, r, a_lr, su, sv, rx, rg) = jnp.split(proj, SPLIT_POINTS, axis=-1)
    ya = shortconv_branch(a_b, a_c, a_x, sc_conv_w)
    yb = gla_branch(q, k, v, r, a_lr, gla_w_alpha, gla_b_alpha, gla_norm_g)
    yc = sgu_branch(su, sv, sgu_ln_g, sgu_ln_b, sgu_w, sgu_b)
    yd = rglru_branch(rx, rg, lru_conv_w, lru_conv_b, lru_w_a, lru_b_a, lru_w_x, lru_b_x, lru_lambda)
    y_stack = jnp.stack([ya, yb, yc, yd], axis=2)
    branch = jnp.einsum('bskc,kcd->bskd', y_stack, w_branch)
    gates = jax.nn.sigmoid((h @ w_gate).reshape(bsz, s, N_BRANCH, D_MODEL) + b_gate)
    merged = jnp.sum(gates * branch, axis=2)
    return merged @ w_mix_out


def cross_attention(h, mem_n, wq, wkv, wo):
    bsz, s, _ = h.shape
    m = mem_n.shape[1]
    q = (h @ wq).reshape(bsz, s, XA_HEADS, XA_HEAD_DIM)
    k, v = jnp.split(mem_n @ wkv, 2, axis=-1)
    k = k.reshape(bsz, m, XA_HEADS, XA_HEAD_DIM)
    v = v.reshape(bsz, m, XA_HEADS, XA_HEAD_DIM)
    sc = jnp.einsum('bshd,bmhd->bhsm', q, k).astype(jnp.float32) * (XA_HEAD_DIM ** -0.5)
    p = jax.nn.softmax(sc, axis=-1).astype(v.dtype)
    o = jnp.einsum('bhsm,bmhd->bshd', p, v).reshape(bsz, s, D_MODEL)
    return o @ wo


def conv_ffn(h, w_up, conv_w, conv_b, w_down):
    up = causal_dwconv(h @ w_up, conv_w, conv_b)
    g, val = jnp.split(up, 2, axis=-1)
    return (jax.nn.gelu(g) * val) @ w_down


def setup_inputs(seed: int = 0) -> dict:
    key = jax.random.key(seed)
    ks = iter(jax.random.split(key, 40))
    L, D, W, G, C = DEPTH, D_MODEL, BRANCH_W, N_GROUPS, GROUP_W

    def nrm(shape, scale):
        return jax.random.normal(next(ks), shape, jnp.float32) * scale

    def gain(shape):
        return 1.0 + nrm(shape, 0.05)

    a8 = jax.random.uniform(next(ks), (L, W), jnp.float32, 0.9, 0.999)
    a_base = a8 ** (1.0 / LRU_C)
    lru_lambda = jnp.log(a_base) - jnp.log1p(-a_base)
    return {
        'x': nrm((BATCH, SEQ, D), 1.0),
        'mem': nrm((BATCH, N_MEM, D), 1.0),
        'norm_g': gain((L, N_NORMS, D)),
        'w_in': nrm((L, D, D_IN), D ** -0.5),
        'sc_conv_w': nrm((L, SHORTCONV_K, W), SHORTCONV_K ** -0.5),
        'gla_w_alpha': nrm((L, GLA_RANK, W), GLA_RANK ** -0.5),
        'gla_b_alpha': nrm((L, W), 0.5),
        'gla_norm_g': gain((L, W)),
        'sgu_ln_g': gain((L, W)),
        'sgu_ln_b': nrm((L, W), 0.02),
        'sgu_w': nrm((L, G, SGU_CHUNK, SGU_CHUNK), SGU_CHUNK ** -0.5),
        'sgu_b': 1.0 + nrm((L, G, SGU_CHUNK), 0.1),
        'lru_conv_w': nrm((L, LRU_CONV_K, W), LRU_CONV_K ** -0.5),
        'lru_conv_b': nrm((L, W), 0.02),
        'lru_w_a': nrm((L, G, C, C), C ** -0.5),
        'lru_b_a': nrm((L, W), 0.1),
        'lru_w_x': nrm((L, G, C, C), C ** -0.5),
        'lru_b_x': nrm((L, W), 0.1),
        'lru_lambda': lru_lambda,
        'w_gate': nrm((L, D, N_BRANCH * D), D ** -0.5),
        'b_gate': nrm((L, N_BRANCH, D), 0.1),
        'w_branch': nrm((L, N_BRANCH, W, D), W ** -0.5),
        'w_mix_out': nrm((L, D, D), D ** -0.5),
        'xa_wq': nrm((L, D, D), D ** -0.5),
        'xa_wkv': nrm((L, D, 2 * D), D ** -0.5),
        'xa_wo': nrm((L, D, D), D ** -0.5),
        'ffn_w_up': nrm((L, D, 2 * D_FF), D ** -0.5),
        'ffn_conv_w': nrm((L, FFN_CONV_K, 2 * D_FF), FFN_CONV_K ** -0.5),
        'ffn_conv_b': nrm((L, 2 * D_FF), 0.02),
        'ffn_w_down': nrm((L, D_FF, D), D_FF ** -0.5),
    }


def reference(x, mem, norm_g, w_in, sc_conv_w, gla_w_alpha, gla_b_alpha, gla_norm_g,
              sgu_ln_g, sgu_ln_b, sgu_w, sgu_b,
              lru_conv_w, lru_conv_b, lru_w_a, lru_b_a, lru_w_x, lru_b_x, lru_lambda,
              w_gate, b_gate, w_branch, w_mix_out, xa_wq, xa_wkv, xa_wo,
              ffn_w_up, ffn_conv_w, ffn_conv_b, ffn_w_down):
    for l in range(DEPTH):
        g = norm_g[l]
        h = rms_norm(x, g[0])
        y = hybrid_mixer(h, w_in[l], sc_conv_w[l], gla_w_alpha[l], gla_b_alpha[l], gla_norm_g[l],
                         sgu_ln_g[l], sgu_ln_b[l], sgu_w[l], sgu_b[l],
                         lru_conv_w[l], lru_conv_b[l], lru_w_a[l], lru_b_a[l], lru_w_x[l], lru_b_x[l],
                         lru_lambda[l], w_gate[l], b_gate[l], w_branch[l], w_mix_out[l])
        x = x + rms_norm(y, g[1])
        h = rms_norm(x, g[2])
        mem_n = rms_norm(mem, g[4])
        y = cross_attention(h, mem_n, xa_wq[l], xa_wkv[l], xa_wo[l])
        x = x + rms_norm(y, g[3])
        h = rms_norm(x, g[5])
        y = conv_ffn(h, ffn_w_up[l], ffn_conv_w[l], ffn_conv_b[l], ffn_w_down[l])
        x = x + rms_norm(y, g[6])
    return x
```

```python
import numpy as np
from contextlib import ExitStack
import concourse.bass as bass
import concourse.mybir as mybir
from concourse.bass_utils import run_bass_kernel_spmd

F32 = mybir.dt.float32
BF16 = mybir.dt.bfloat16
ALU = mybir.AluOpType
AF = mybir.ActivationFunctionType

D = 1024
KC = 8
T = 512
NB = 4
EPS = 1e-6
DFF = 2816
NFC = 44
NLINK = 2 * 128 + 6

C_G = 0
C_GNG = 56
C_LCB = 58
C_LBA = 60
C_LBX = 62
C_LAM = 64
C_BG = 66
C_FCB = 98
NV = 142

WI_Q, WI_RX, WI_K, WI_V, WI_AB, WI_AC, WI_AX, WI_R, WI_SU, WI_SV, WI_RG, WI_ALR = (
    0, 256, 512, 768, 1024, 1280, 1536, 1792, 2048, 2304, 2560, 2816)


class Buf:
    __slots__ = ("w", "r", "excl")

    def __init__(self, excl=False):
        self.w = None
        self.r = {}
        self.excl = excl


class Tl:
    __slots__ = ("ap", "bufs", "cb")

    def __init__(self, ap, bufs, cb=None):
        self.ap = ap
        self.bufs = tuple(bufs)
        self.cb = cb

    def __getitem__(self, key):
        return Tl(self.ap[key], self.bufs)

    def c(self, i):
        return Tl(self.ap[:, i], (self.cb[i],))

    def cs(self, i, key):
        return Tl(self.ap[:, i][key], (self.cb[i],))


class Sched:
    ENG = ("pe", "act", "dve", "pool", "sp")

    def __init__(self, nc, stack):
        self.nc = nc
        self.stack = stack
        self.sems = {}
        self.cnt = {}
        self.waited = {k: {} for k in self.ENG}
        self.prog = {k: [] for k in self.ENG}
        for k in self.ENG:
            self.sems[k] = stack.enter_context(nc.semaphore("s_" + k))
            self.cnt[k] = 0

    def _deps(self, eng, reads, writes):
        deps = {}

        def add(ev, same_ok):
            if ev is None:
                return
            k, v = ev
            if k == eng and same_ok:
                return
            if deps.get(k, 0) < v:
                deps[k] = v

        for b in reads:
            add(b.w, eng == "pe")
            if b.excl:
                for k, v in b.r.items():
                    add((k, v), True)
        for b in writes:
            add(b.w, True)
            for k, v in b.r.items():
                add((k, v), True)
        out = []
        wd = self.waited[eng]
        for k, v in deps.items():
            if wd.get(k, 0) < v:
                wd[k] = v
                out.append((k, v))
        return out

    def _commit(self, ev, reads, writes):
        k, v = ev
        for b in writes:
            b.w = ev
            b.r = {}
        for b in reads:
            if b.r.get(k, 0) < v:
                b.r[k] = v

    def op(self, eng, fn, reads=(), writes=()):
        waits = self._deps(eng, reads, writes)
        self.cnt[eng] += 1
        self.prog[eng].append((waits, fn, eng, 1))
        self._commit((eng, self.cnt[eng]), reads, writes)

    def dma(self, q, out_ap, in_ap, reads=(), writes=(), sem=None):
        if sem not in self.sems:
            self.sems[sem] = self.stack.enter_context(self.nc.semaphore("d_" + sem))
            self.cnt[sem] = 0
        waits = self._deps(q, reads, writes)
        self.cnt[sem] += 16
        self.prog[q].append((waits, lambda e: e.dma_start(out=out_ap, in_=in_ap), sem, 16))
        self._commit((sem, self.cnt[sem]), reads, writes)

    def wait_all(self, eng, bufs):
        waits = self._deps(eng, bufs, ())
        self.prog[eng].append((waits, None, None, 0))

    def emit(self, block):
        sems = self.sems
        prog = self.prog

        def run(e, lst):
            for waits, fn, semk, inc in lst:
                for k, v in waits:
                    e.wait_ge(sems[k], v)
                if fn is not None:
                    fn(e).then_inc(sems[semk], inc)

        @block.tensor
        def _(e):
            run(e, prog["pe"])

        @block.scalar
        def _(e):
            run(e, prog["act"])

        @block.vector
        def _(e):
            run(e, prog["dve"])

        @block.gpsimd
        def _(e):
            run(e, prog["pool"])

        @block.sync
        def _(e):
            run(e, prog["sp"])


class StopBuild(Exception):
    pass


STOP = [99]
SKIP_XA = [False]


class KB:
    def __init__(self, nc, st):
        self.nc = nc
        self.st = st
        self.S = Sched(nc, st)
        self.uid = 0
        self.psb = []
        self.psi = 0
        self.held = set()
        self.wsem = 0

    def ck(self, n):
        if n >= STOP[0]:
            raise StopBuild()

    def name(self, p):
        self.uid += 1
        return "%s%d" % (p, self.uid)

    def sb(self, shape, dt, nchunk=None):
        t = self.st.enter_context(self.nc.sbuf_tensor(self.name("t"), list(shape), dt))
        if nchunk:
            cb = [Buf() for _ in range(nchunk)]
            return Tl(t[:], cb, cb)
        return Tl(t[:], [Buf()])

    def dram_in(self, name, shape, dt=F32):
        return Tl(self.nc.dram_tensor(name, list(shape), dt, kind="ExternalInput").ap(), [Buf()])

    def dram_out(self, name, shape, dt=F32):
        return Tl(self.nc.dram_tensor(name, list(shape), dt, kind="ExternalOutput").ap(), [Buf()])

    def init_psum(self):
        for i in range(7):
            t = self.st.enter_context(self.nc.psum_tensor(self.name("ps"), [128, 512], F32))
            self.psb.append(Tl(t[:], [Buf(excl=True)]))
        t = self.st.enter_context(self.nc.psum_tensor(self.name("pst"), [128, 1024], BF16))
        self.pst = Tl(t[:], [Buf(excl=True)])

    def ps(self, hold=False):
        for _ in range(8):
            i = self.psi
            self.psi = (self.psi + 1) % 7
            if i not in self.held:
                if hold:
                    self.held.add(i)
                return self.psb[i]
        raise RuntimeError("no psum")

    def release(self, t):
        for i, p in enumerate(self.psb):
            if p.bufs[0] is t.bufs[0]:
                self.held.discard(i)

    @staticmethod
    def _rb(*ts):
        out = []
        for t in ts:
            if isinstance(t, Tl):
                out.extend(t.bufs)
        return out

    def mm(self, out, lhsT, rhs, start=True, stop=True, tp=None):
        if tp is None:
            fn = lambda e: e.matmul(out.ap, lhsT=lhsT.ap, rhs=rhs.ap, start=start, stop=stop)
        else:
            fn = lambda e: e.matmul(out.ap, lhsT=lhsT.ap, rhs=rhs.ap, start=start, stop=stop, tile_position=tp)
        self.S.op("pe", fn, reads=self._rb(lhsT, rhs), writes=self._rb(out))

    def transpose(self, out, in_, ident):
        self.S.op("pe", lambda e: e.transpose(out.ap, in_.ap, ident.ap),
                  reads=self._rb(in_, ident), writes=self._rb(out))

    def act(self, out, in_, func, bias=None, scale=None, accum=None):
        kw = {}
        if bias is not None:
            kw["bias"] = bias.ap if isinstance(bias, Tl) else bias
        if scale is not None:
            kw["scale"] = scale.ap if isinstance(scale, Tl) else scale
        if accum is not None:
            kw["accum_out"] = accum.ap
        if func == AF.Copy and (isinstance(bias, Tl) or isinstance(scale, Tl)):
            func = AF.Identity
        self.S.op("act", lambda e: e.activation(out=out.ap, in_=in_.ap, func=func, **kw),
                  reads=self._rb(in_, bias, scale), writes=self._rb(out, accum))

    def tt(self, out, a, b, op, eng="dve"):
        self.S.op(eng, lambda e: e.tensor_tensor(out=out.ap, in0=a.ap, in1=b.ap, op=op),
                  reads=self._rb(a, b), writes=self._rb(out))

    def stt(self, out, in0, scalar, in1, op0, op1):
        sc = scalar.ap if isinstance(scalar, Tl) else scalar
        self.S.op("dve", lambda e: e.scalar_tensor_tensor(out=out.ap, in0=in0.ap, scalar=sc, in1=in1.ap, op0=op0, op1=op1),
                  reads=self._rb(in0, scalar, in1), writes=self._rb(out))

    def ts(self, out, in0, s1, s2, op0, op1=None, eng="dve"):
        a1 = s1.ap if isinstance(s1, Tl) else s1
        a2 = s2.ap if isinstance(s2, Tl) else s2
        if op1 is None:
            fn = lambda e: e.tensor_scalar(out=out.ap, in0=in0.ap, scalar1=a1, scalar2=None, op0=op0)
        else:
            fn = lambda e: e.tensor_scalar(out=out.ap, in0=in0.ap, scalar1=a1, scalar2=a2, op0=op0, op1=op1)
        self.S.op(eng, fn, reads=self._rb(in0, s1, s2), writes=self._rb(out))

    def copy(self, out, in_, eng="dve"):
        self.S.op(eng, lambda e: e.tensor_copy(out=out.ap, in_=in_.ap), reads=self._rb(in_), writes=self._rb(out))

    def recip(self, out, in_):
        self.S.op("dve", lambda e: e.reciprocal(out=out.ap, in_=in_.ap), reads=self._rb(in_), writes=self._rb(out))

    def scan(self, out, d0, d1, init, op0, op1):
        ini = init.ap if isinstance(init, Tl) else init
        self.S.op("dve", lambda e: e.tensor_tensor_scan(out=out.ap, data0=d0.ap, data1=d1.ap, initial=ini, op0=op0, op1=op1),
                  reads=self._rb(d0, d1, init), writes=self._rb(out))

    def memset(self, out, val, eng="pool"):
        self.S.op(eng, lambda e: e.memset(out.ap, val), writes=self._rb(out))

    def aselect(self, t, pattern, cmp, base, cm):
        self.S.op("pool", lambda e: e.affine_select(out=t.ap, in_=t.ap, pattern=pattern, compare_op=cmp, fill=0.0,
                                                    base=base, channel_multiplier=cm),
                  reads=self._rb(t), writes=self._rb(t))

    def dma(self, out, in_, q="sp", sem=None):
        if sem is None:
            self.wsem += 1
            sem = "x%d" % self.wsem
        self.S.dma(q, out.ap, in_.ap, reads=self._rb(in_), writes=self._rb(out), sem=sem)

    def consts(self):
        self.identf = self.sb([128, 128], F32)
        self.memset(self.identf, 1.0)
        self.aselect(self.identf, [[-1, 128]], ALU.is_equal, 0, 1)
        self.ident = self.sb([128, 128], BF16)
        self.copy(self.ident, self.identf)
        self.ones = self.sb([128, 128], BF16)
        self.memset(self.ones, 1.0)
        self.zeros = self.sb([128, 512], F32)
        self.memset(self.zeros, 0.0)

    def init_ring(self, nslots=4):
        self.ring = [self.sb([128, 4096], BF16) for _ in range(nslots)]
        self.ri = 0

    def wload(self, dram, rows, cols, r0=0, c0=0):
        slot = self.ring[self.ri]
        i = self.ri
        self.ri = (self.ri + 1) % len(self.ring)
        view = Tl(slot.ap[:, 0:rows * cols].rearrange("p (r c) -> p r c", c=cols), slot.bufs)
        self.S.dma("pool", view.ap, dram.ap[:, r0:r0 + rows, c0:c0 + cols], reads=self._rb(dram), writes=self._rb(view),
                   sem="ring%d" % i)
        return view

    def rmsnorm_h(self, h, xs, n, gcol, cv, sq, rstd_t):
        p = self.ps()
        for kc in range(KC):
            self.act(sq.cs(kc, (slice(None), slice(0, n))), xs[kc], AF.Square)
        for kc in range(KC):
            self.mm(p[:, 0:n], self.ones, sq.cs(kc, (slice(None), slice(0, n))), start=(kc == 0), stop=(kc == KC - 1))
        self.act(rstd_t[:, 0:n], p[:, 0:n], AF.Sqrt, bias=EPS, scale=1.0 / D)
        self.recip(rstd_t[:, 0:n], rstd_t[:, 0:n])
        for kc in range(KC):
            self.stt(h.cs(kc, (slice(None), slice(0, n))), xs[kc], cv[:, gcol + kc:gcol + kc + 1], rstd_t[:, 0:n], ALU.mult, ALU.mult)

    def proj(self, p, w, c0, m, h, n):
        for kc in range(KC):
            self.mm(p[0:m, 0:n], w[:, kc, c0:c0 + m], h.cs(kc, (slice(None), slice(0, n))), start=(kc == 0), stop=(kc == KC - 1))

    def postnorm_residual(self, xs, ybuf, gcol, cv, sq, rstd_t, tmp):
        p = self.ps()
        for kc in range(KC):
            self.act(sq.c(kc), ybuf.c(kc), AF.Square)
        for kc in range(KC):
            self.mm(p, self.ones, sq.c(kc), start=(kc == 0), stop=(kc == KC - 1))
        self.act(rstd_t, p, AF.Sqrt, bias=EPS, scale=1.0 / D)
        self.recip(rstd_t, rstd_t)
        for kc in range(KC):
            self.stt(tmp, ybuf.c(kc), cv[:, gcol + kc:gcol + kc + 1], rstd_t, ALU.mult, ALU.mult)
            self.tt(xs[kc], xs[kc], tmp, ALU.add)

    def finish(self, outs):
        bufs = []
        for o in outs:
            bufs.extend(o.bufs)
        self.S.wait_all("sp", bufs)
        with self.nc.Block() as block:
            self.S.emit(block)


def build_A(NT):
    NTOK = NT * T
    nc = bass.Bass("TRN2", target_bir_lowering=False)
    with ExitStack() as st:
        k = KB(nc, st)
        xT = k.dram_in("xT", [128, KC, NTOK])
        xh = k.dram_in("xh", [128, KC, 4])
        cvd = k.dram_in("cvec", [128, NV])
        w_in = k.dram_in("w_in", [128, KC, 2832])
        dlru = k.dram_in("diag_lru", [128, 8, 128])
        bda = k.dram_in("bd_a", [128, 2, 128])
        bdx = k.dram_in("bd_x", [128, 2, 128])
        wal = k.dram_in("walpha", [16, 256])
        bal = k.dram_in("balpha", [1, 256])
        o_opart = k.dram_out("opart", [128, 2, NTOK])
        o_qtil = k.dram_out("qtil", [128, 2, NTOK])
        o_hloc = k.dram_out("hloc", [128, 2, NTOK])
        o_pcum = k.dram_out("pcum", [128, 2, NTOK])
        o_link = k.dram_out("link", [128, NLINK])

        try:
            k.init_psum()
            k.consts()
            k.init_ring(4)
            cv = k.sb([128, NV], F32)
            k.dma(cv, cvd)
            dl = k.sb([128, 8, 128], BF16); k.dma(dl, dlru, q="pool")
            wa = k.sb([128, 2, 128], BF16); k.dma(wa, bda, q="pool")
            wx = k.sb([128, 2, 128], BF16); k.dma(wx, bdx, q="pool")
            walb = k.sb([16, 256], BF16); k.dma(walb, wal, q="pool")
            balb = k.sb([1, 256], BF16); k.dma(balb, bal, q="pool")
            trib = k.sb([128, 128], F32)
            k.memset(trib, 1.0)
            k.aselect(trib, [[1, 128]], ALU.is_ge, 0, -1)
            k.aselect(trib[:, 64:128], [[0, 64]], ALU.is_ge, -64, 1)
            mask4 = k.sb([128, 4, 128], BF16)
            for b in range(4):
                k.copy(mask4[:, b, :], trib)
            mask4f = Tl(mask4.ap.rearrange("p b i -> p (b i)"), mask4.bufs)
            cneg = k.sb([128, 4], F32)
            k.act(cneg[:, 0:2], cv[:, C_LAM:C_LAM + 2], AF.Exp, scale=-1.0)
            k.act(cneg[:, 0:2], cneg[:, 0:2], AF.Ln, bias=1.0)
            k.ts(cneg[:, 2:4], cneg[:, 0:2], -16.0, None, ALU.mult)
            k.ts(cneg[:, 0:2], cneg[:, 0:2], -8.0, None, ALU.mult)

            x = k.sb([128, KC, T], F32, nchunk=KC)
            h = k.sb([128, KC, T], BF16, nchunk=KC)
            sq = k.sb([128, KC, T], BF16, nchunk=KC)
            rstd = k.sb([128, T], F32)
            xhs = k.sb([128, KC, 4], F32, nchunk=KC)
            hh = k.sb([128, KC, 4], BF16, nchunk=KC)
            rx_ext = k.sb([128, 2, 3 + T], BF16, nchunk=2)
            S_st = k.sb([128, 2, 128], F32, nchunk=2)
            k.memset(S_st, 0.0)
            Dprev = k.sb([128, 2], F32); k.memset(Dprev, 1.0)
            hprev = k.sb([128, 2], F32); k.memset(hprev, 0.0)
            pprev = k.sb([128, 2], F32); k.memset(pprev, 1.0)

            k.dma(xhs, xh)
            k.rmsnorm_h(hh, [xhs.c(kc) for kc in range(KC)], 4, C_G + 0, cv, sq, rstd)
            wg0 = k.wload(w_in, KC, 512, 0, 0)
            for c in range(2):
                p = k.ps()
                k.proj(p, wg0, 256 + c * 128, 128, hh, 4)
                k.act(rx_ext.cs(c, (slice(None), slice(0, 3))), p[:, 1:4], AF.Copy)

            alr = k.sb([16, T], BF16)
            nsp = k.sb([128, 4, 256], F32, nchunk=4)
            nhi = k.sb([128, 4, 256], BF16, nchunk=4)
            nlo = k.sb([128, 4, 256], BF16, nchunk=4)
            tribb = k.sb([128, 128], BF16)
            k.copy(tribb, trib)
            ek = k.sb([128, 4, 256], F32, nchunk=4)
            kdec = k.sb([128, 4, 256], BF16, nchunk=4)
            vtok = k.sb([128, 4, 256], BF16, nchunk=4)
            kdecT = k.sb([128, 2, T], BF16, nchunk=2)
            eq = k.sb([128, 2, T], F32, nchunk=2)
            qdecT = k.sb([128, 2, T], BF16, nchunk=2)
            scT = k.sb([128, 4, T], BF16, nchunk=4)
            kvs = k.sb([128, 8, 2, 128], F32, nchunk=8)
            Sb = k.sb([128, 8, 2, 128], BF16, nchunk=8)
            opart = k.sb([128, 2, T], F32, nchunk=2)
            qtil = k.sb([128, 2, T], F32, nchunk=2)
            Dinc = k.sb([128, 2, 9], F32, nchunk=2)
            hl = k.sb([128, 2, T], F32, nchunk=2)
            pc = k.sb([128, 2, T], F32, nchunk=2)
            xc = k.sb([128, T], F32)
            xcb = k.sb([128, T], BF16)
            rg_ = k.sb([128, T], F32)
            ig_ = k.sb([128, T], F32)
            a_ = k.sb([128, T], F32)
            m_ = k.sb([128, T], F32)
            k.ck(1)
            for it in range(NT):
                t0 = it * T
                k.dma(x, xT[:, :, t0:t0 + T], sem="xld")
                k.rmsnorm_h(h, [x.c(kc) for kc in range(KC)], T, C_G + 0, cv, sq, rstd)
                if it > 0:
                    wg0 = k.wload(w_in, KC, 512, 0, 0)
                wg1 = k.wload(w_in, KC, 512, 0, 512)
                walr = k.wload(w_in, KC, 16, 0, WI_ALR)
                p = k.ps()
                k.proj(p, walr, 0, 16, h, T)
                k.act(alr, p[0:16, :], AF.Copy)
                for b2 in range(2):
                    p = k.ps()
                    for bb in range(2):
                        b = b2 * 2 + bb
                        k.mm(p[:, bb * 256:(bb + 1) * 256], alr[:, b * 128:(b + 1) * 128], walb, start=True, stop=False)
                        k.mm(p[:, bb * 256:(bb + 1) * 256], k.ones[0:1, :], balb, start=False, stop=True)
                    for bb in range(2):
                        b = b2 * 2 + bb
                        k.act(ek.c(b), p[:, bb * 256:(bb + 1) * 256], AF.Exp, scale=-1.0)
                        k.act(nsp.c(b), ek.c(b), AF.Ln, bias=1.0)
                k.ck(2)
                for b2 in range(2):
                    p = k.ps()
                    for bb in range(2):
                        b = b2 * 2 + bb
                        k.copy(nhi.c(b), nsp.c(b))
                        k.tt(nlo.c(b), nsp.c(b), nhi.c(b), ALU.subtract)
                        k.mm(p[:, bb * 256:(bb + 1) * 256], tribb, nhi.c(b), start=True, stop=False)
                        k.mm(p[:, bb * 256:(bb + 1) * 256], tribb, nlo.c(b), start=False, stop=True)
                    for bb in range(2):
                        b = b2 * 2 + bb
                        k.act(ek.c(b), p[:, bb * 256:(bb + 1) * 256], AF.Exp, scale=1.0 / 16.0)
                k.ck(2.5)
                for b in range(4):
                    p = k.ps()
                    for kc in range(KC):
                        k.mm(p, h.cs(kc, (slice(None), slice(b * 128, (b + 1) * 128))), wg1[:, kc, :], start=(kc == 0), stop=(kc == KC - 1))
                    k.ck(2.6)
                    k.tt(kdec.c(b), p[:, 0:256], ek.c(b), ALU.mult)
                    k.ck(2.7)
                    k.act(vtok.c(b), p[:, 256:512], AF.Copy)
                    k.ck(2.8)
                k.ck(3)
                for c in range(2):
                    for b in range(4):
                        k.transpose(k.pst[:, b * 128:(b + 1) * 128], kdec.cs(b, (slice(None), slice(c * 128, (c + 1) * 128))), k.ident)
                    k.copy(kdecT.c(c), k.pst[:, 0:T])
                for c in range(2):
                    p = k.ps()
                    for b in range(4):
                        k.mm(p[:, b * 128:(b + 1) * 128], nhi.cs(b, (slice(None), slice(c * 128, (c + 1) * 128))), tribb, start=True, stop=False)
                        k.mm(p[:, b * 128:(b + 1) * 128], nlo.cs(b, (slice(None), slice(c * 128, (c + 1) * 128))), tribb, start=False, stop=True)
                    k.act(eq.c(c), p, AF.Exp, scale=-1.0 / 16.0)
                for c in range(2):
                    p = k.ps()
                    k.proj(p, wg0, c * 128, 128, h, T)
                    k.stt(qdecT.c(c), p, 0.125, eq.c(c), ALU.mult, ALU.mult)
                for c in range(2):
                    p = k.ps()
                    k.proj(p, wg0, 256 + c * 128, 128, h, T)
                    k.act(rx_ext.cs(c, (slice(None), slice(3, 3 + T))), p, AF.Copy)
                k.ck(4)
                for hd in range(4):
                    c, r0 = hd // 2, (hd % 2) * 64
                    p = k.ps()
                    for b in range(4):
                        k.mm(p[:, b * 128:(b + 1) * 128], kdecT.cs(c, (slice(r0, r0 + 64), slice(b * 128, (b + 1) * 128))),
                             qdecT.cs(c, (slice(r0, r0 + 64), slice(b * 128, (b + 1) * 128))), start=True, stop=True)
                    k.tt(scT.c(hd), p, mask4f, ALU.mult)
                k.ck(4.2)
                for c in range(2):
                    for par in range(2):
                        p = k.ps()
                        r = par * 64
                        for j in range(4):
                            n = 2 * j + par
                            k.mm(p[:, j * 128:(j + 1) * 128], kdec.cs(j, (slice(r, r + 64), slice(c * 128, (c + 1) * 128))),
                                 vtok.cs(j, (slice(r, r + 64), slice(c * 128, (c + 1) * 128))), start=True, stop=True)
                        for j in range(4):
                            n = 2 * j + par
                            k.ts(Tl(kvs.ap[:, n, c, :], (kvs.cb[n],)), p[:, j * 128:(j + 1) * 128],
                                 eq.cs(c, (slice(None), slice(n * 64 + 63, n * 64 + 64))), None, ALU.mult)
                k.ck(4.5)
                for n in range(8):
                    for c in range(2):
                        k.act(Tl(Sb.ap[:, n, c, :], (Sb.cb[n],)), S_st.c(c), AF.Copy)
                        k.stt(S_st.c(c), S_st.c(c), eq.cs(c, (slice(None), slice(n * 64 + 63, n * 64 + 64))),
                              Tl(kvs.ap[:, n, c, :], (kvs.cb[n],)), ALU.mult, ALU.add)
                k.ck(5)
                for c in range(2):
                    p = k.ps()
                    for hh_ in range(2):
                        hd = c * 2 + hh_
                        hr = hh_ * 64
                        for b in range(4):
                            k.mm(p[hr:hr + 64, b * 128:(b + 1) * 128], vtok.cs(b, (slice(None), slice(hd * 64, hd * 64 + 64))),
                                 scT.cs(hd, (slice(None), slice(b * 128, (b + 1) * 128))), start=True, stop=False, tp=(0, hr))
                            for n in (2 * b, 2 * b + 1):
                                k.mm(p[hr:hr + 64, n * 64:(n + 1) * 64], Tl(Sb.ap[hr:hr + 64, n, c, hr:hr + 64], (Sb.cb[n],)),
                                     qdecT.cs(c, (slice(hr, hr + 64), slice(n * 64, (n + 1) * 64))), start=False, stop=True, tp=(hr, hr))
                    k.act(opart.c(c), p, AF.Copy)
                k.dma(o_opart[:, :, t0:t0 + T], opart, sem="st_op")
                for c in range(2):
                    k.copy(Dinc.cs(c, (slice(None), slice(0, 1))), Dprev[:, c:c + 1])
                    k.scan(Dinc.cs(c, (slice(None), slice(1, 9))), eq.cs(c, (slice(None), slice(63, T, 64))), k.zeros[:, 0:8],
                           Dprev[:, c:c + 1], ALU.mult, ALU.add)
                    k.tt(Tl(qtil.ap[:, c, :].rearrange("p (n i) -> p n i", i=64), (qtil.cb[c],)),
                         Tl(qdecT.ap[:, c, :].rearrange("p (n i) -> p n i", i=64), (qdecT.cb[c],)),
                         Tl(Dinc.ap[:, c, 0:8].unsqueeze(2).to_broadcast([128, 8, 64]), (Dinc.cb[c],)), ALU.mult)
                    k.copy(Dprev[:, c:c + 1], Dinc.cs(c, (slice(None), slice(8, 9))))
                k.dma(o_qtil[:, :, t0:t0 + T], qtil, sem="st_qt")

                k.ck(6)
                for c in range(2):
                    p = k.ps()
                    for tap in range(4):
                        k.mm(p, dl[:, tap * 2 + c, :], rx_ext.cs(c, (slice(None), slice(tap, tap + T))), start=(tap == 0), stop=(tap == 3))
                    k.act(xc, p, AF.Identity, bias=cv[:, C_LCB + c:C_LCB + c + 1])
                    k.act(xcb, p, AF.Identity, bias=cv[:, C_LCB + c:C_LCB + c + 1])
                    pr = k.ps()
                    k.mm(pr, wa[:, c, :], xcb)
                    pi = k.ps()
                    k.mm(pi, wx[:, c, :], xcb)
                    k.act(rg_, pr, AF.Sigmoid, bias=cv[:, C_LBA + c:C_LBA + c + 1])
                    k.act(ig_, pi, AF.Sigmoid, bias=cv[:, C_LBX + c:C_LBX + c + 1])
                    k.act(a_, rg_, AF.Exp, scale=cneg[:, c:c + 1])
                    k.act(m_, rg_, AF.Exp, scale=cneg[:, 2 + c:3 + c])
                    k.act(m_, m_, AF.Sqrt, scale=-1.0, bias=1.0)
                    k.tt(ig_, ig_, xc, ALU.mult)
                    k.tt(ig_, ig_, m_, ALU.mult)
                    k.scan(hl.c(c), a_, ig_, hprev[:, c:c + 1], ALU.mult, ALU.add)
                    k.scan(pc.c(c), a_, k.zeros, pprev[:, c:c + 1], ALU.mult, ALU.add)
                    k.copy(hprev[:, c:c + 1], hl.cs(c, (slice(None), slice(T - 1, T))))
                    k.copy(pprev[:, c:c + 1], pc.cs(c, (slice(None), slice(T - 1, T))))
                    k.copy(rx_ext.cs(c, (slice(None), slice(0, 3))), rx_ext.cs(c, (slice(None), slice(T, T + 3))), eng="pool")
                k.dma(o_hloc[:, :, t0:t0 + T], hl, sem="st_hl")
                k.dma(o_pcum[:, :, t0:t0 + T], pc, sem="st_pc")

            k.ck(7)
            link = k.sb([128, NLINK], F32)
            k.copy(link[:, 0:256], Tl(S_st.ap.rearrange("p c v -> p (c v)"), S_st.bufs))
            k.copy(link[:, 256:258], Dprev)
            k.copy(link[:, 258:260], hprev)
            k.copy(link[:, 260:262], pprev)
            k.dma(o_link, link)

        except StopBuild:
            pass
        k.finish([o_opart, o_qtil, o_hloc, o_pcum, o_link])
    return nc


def build_B(NT):
    NTOK = NT * T
    nc = bass.Bass("TRN2", target_bir_lowering=False)
    with ExitStack() as st:
        k = KB(nc, st)
        xT = k.dram_in("xT", [128, KC, NTOK])
        xh = k.dram_in("xh", [128, KC, 4])
        cvd = k.dram_in("cvec", [128, NV])
        w_in = k.dram_in("w_in", [128, KC, 2832])
        w_gate = k.dram_in("w_gate", [128, KC, 4096])
        w_br = k.dram_in("w_br", [128, 8, 1024])
        w_mix = k.dram_in("w_mix", [128, KC, 1024])
        w_q = k.dram_in("w_q", [128, KC, 1024])
        w_kv = k.dram_in("w_kv", [128, KC, 2048])
        w_o = k.dram_in("w_o", [128, KC, 1024])
        memT = k.dram_in("memT", [128, KC, 256])
        dscd = k.dram_in("diag_sc", [128, 6, 128])
        wmd = k.dram_in("WmT", [128, 4, 128])
        lnGd = k.dram_in("lnG", [128, 256])
        lnBd = k.dram_in("lnB", [128, 256])
        bsd = k.dram_in("bsT", [128, 2, 128])
        i_opart = k.dram_in("opart", [128, 2, NTOK])
        i_qtil = k.dram_in("qtil", [128, 2, NTOK])
        i_hloc = k.dram_in("hloc", [128, 2, NTOK])
        i_pcum = k.dram_in("pcum", [128, 2, NTOK])
        lkd = k.dram_in("links", [128, 8, NLINK])
        seld = k.dram_in("sel", [128, 24])
        xo = k.dram_out("xo", [128, KC, NTOK])

        k.init_psum()
        k.consts()
        k.init_ring(4)
        cv = k.sb([128, NV], F32); k.dma(cv, cvd)
        dsc = k.sb([128, 6, 128], BF16); k.dma(dsc, dscd, q="pool")
        wmf = k.sb([128, 4, 128], F32); k.dma(wmf, wmd)
        lnG = k.sb([128, 256], F32); k.dma(lnG, lnGd)
        lnB = k.sb([128, 256], F32); k.dma(lnB, lnBd)
        bsT = k.sb([128, 2, 128], F32); k.dma(bsT, bsd)
        lk = k.sb([128, 8, NLINK], F32); k.dma(lk, lkd)
        sl = k.sb([128, 24], F32); k.dma(sl, seld)
        triu = k.sb([128, 128], F32)
        k.memset(triu, 1.0)
        k.aselect(triu, [[1, 128]], ALU.is_ge, 0, -1)
        wm = k.sb([128, 4, 128], BF16)
        for g in range(4):
            k.tt(wm[:, g, :], wmf[:, g, :], triu, ALU.mult)
        onesbd = k.sb([128, 128], BF16)
        k.memset(onesbd, 1.0)
        k.aselect(onesbd[:, 0:64], [[0, 64]], ALU.is_ge, 63, -1)
        k.aselect(onesbd[:, 64:128], [[0, 64]], ALU.is_ge, -64, 1)

        Lj = k.sb([128, NLINK], F32)
        St = k.sb([128, 2, 128], F32, nchunk=2)
        hi = k.sb([128, 2], F32)
        k.memset(St, 0.0)
        k.memset(hi, 0.0)
        for j in range(3):
            k.ts(Lj, lk[:, 0, :], sl[:, j * 8:j * 8 + 1], None, ALU.mult)
            for r in range(1, 8):
                k.stt(Lj, lk[:, r, :], sl[:, j * 8 + r:j * 8 + r + 1], Lj, ALU.mult, ALU.add)
            for c in range(2):
                k.stt(St.c(c), St.c(c), Lj[:, 256 + c:257 + c], Lj[:, c * 128:(c + 1) * 128], ALU.mult, ALU.add)
            k.tt(hi, hi, Lj[:, 260:262], ALU.mult)
            k.tt(hi, hi, Lj[:, 258:260], ALU.add)
        Sib = k.sb([128, 2, 128], BF16)
        k.copy(Sib, Tl(St.ap, St.bufs))

        x = k.sb([128, KC, T], F32, nchunk=KC)
        h = k.sb([128, KC, T], BF16, nchunk=KC)
        sq = k.sb([128, KC, T], BF16, nchunk=KC)
        rstd = k.sb([128, T], F32)
        tmp = k.sb([128, T], F32)
        ybuf = k.sb([128, KC, T], F32, nchunk=KC)
        xhs = k.sb([128, KC, 4], F32, nchunk=KC)
        hh = k.sb([128, KC, 4], BF16, nchunk=KC)
        u_ext = k.sb([128, 2, 2 + T], BF16, nchunk=2)
        xs = [x.c(kc) for kc in range(KC)]

        mem = Tl(ybuf.ap[:, :, 0:256], ybuf.cb, ybuf.cb)
        memn = k.sb([128, KC, 256], BF16, nchunk=KC)
        k.dma(mem, memT)
        k.rmsnorm_h(memn, [mem.c(kc) for kc in range(KC)], 256, C_G + 4 * 8, cv, sq, rstd)
        KT = k.sb([128, KC, 256], BF16, nchunk=KC)
        Vt = k.sb([128, 2, 1024], BF16, nchunk=2)
        for g in range(2):
            wk = k.wload(w_kv, KC, 512, 0, g * 512)
            for j in range(4):
                p = k.ps()
                k.proj(p, wk, j * 128, 128, memn, 256)
                k.act(KT.c(g * 4 + j), p[:, 0:256], AF.Copy)
        for g in range(2):
            wv = k.wload(w_kv, KC, 512, 0, 1024 + g * 512)
            for mc in range(2):
                p = k.ps()
                for kc in range(KC):
                    k.mm(p, memn.cs(kc, (slice(None), slice(mc * 128, (mc + 1) * 128))), wv[:, kc, :], start=(kc == 0), stop=(kc == KC - 1))
                k.act(Vt.cs(mc, (slice(None), slice(g * 512, (g + 1) * 512))), p, AF.Copy)

        k.dma(xhs, xh)
        k.rmsnorm_h(hh, [xhs.c(kc) for kc in range(KC)], 4, C_G + 0, cv, sq, rstd)
        wg2 = k.wload(w_in, KC, 512, 0, WI_AB)
        wg3 = k.wload(w_in, KC, 512, 0, WI_AX)
        acs = k.sb([128, T], F32)
        abs_ = k.sb([128, T], F32)
        for c in range(2):
            p = k.ps()
            k.proj(p, wg2, 256 + c * 128, 128, hh, 4)
            k.act(acs[:, 0:4], p[:, 0:4], AF.Copy)
            p2 = k.ps()
            k.proj(p2, wg3, c * 128, 128, hh, 4)
            k.tt(u_ext.cs(c, (slice(None), slice(0, 2))), p2[:, 2:4], acs[:, 2:4], ALU.mult)

        ya = k.sb([128, 2, T], BF16, nchunk=2)
        yb = k.sb([128, 2, T], BF16, nchunk=2)
        yc = k.sb([128, 2, T], BF16, nchunk=2)
        yd = k.sb([128, 2, T], BF16, nchunk=2)
        opart = k.sb([128, 2, T], F32, nchunk=2)
        qtil = k.sb([128, 2, T], F32, nchunk=2)
        qtb = k.sb([128, 2, T], BF16, nchunk=2)
        hloc = k.sb([128, 2, T], F32, nchunk=2)
        pcum = k.sb([128, 2, T], F32, nchunk=2)
        sr = k.sb([128, T], F32)
        o32 = k.sb([128, T], F32)
        sqb = k.sb([128, T], BF16)
        rs = k.sb([128, T], F32)
        t1 = k.sb([128, T], F32)
        sus = k.sb([128, 2, T], F32, nchunk=2)
        sv32 = k.sb([128, 256], F32)
        junk = k.sb([128, 256], F32)
        st4 = k.sb([128, 4], F32)
        vn = k.sb([128, 256], F32)
        vnb = [k.sb([128, 256], BF16) for _ in range(2)]
        m1 = t1
        gg = sr
        hf = o32
        gts = [k.sb([128, T], F32) for _ in range(2)]
        prods = [k.sb([128, T], BF16) for _ in range(2)]
        merged = k.sb([128, KC, T], BF16, nchunk=KC)
        qT = merged
        oT = k.sb([128, KC, T], BF16, nchunk=KC)
        E = k.sb([128, 2, T], BF16, nchunk=2)
        rden = k.sb([128, T], F32)

        for it in range(NT):
            t0 = it * T
            k.dma(x, xT[:, :, t0:t0 + T], sem="xld")
            k.dma(opart, i_opart[:, :, t0:t0 + T], sem="ld_op")
            k.dma(qtil, i_qtil[:, :, t0:t0 + T], sem="ld_qt")
            k.dma(hloc, i_hloc[:, :, t0:t0 + T], sem="ld_hl")
            k.dma(pcum, i_pcum[:, :, t0:t0 + T], sem="ld_pc")
            k.rmsnorm_h(h, xs, T, C_G + 0, cv, sq, rstd)
            if it > 0:
                wg2 = k.wload(w_in, KC, 512, 0, WI_AB)
                wg3 = k.wload(w_in, KC, 512, 0, WI_AX)
            for c in range(2):
                p = k.ps()
                k.proj(p, wg2, 256 + c * 128, 128, h, T)
                k.act(acs, p, AF.Copy)
                p2 = k.ps()
                k.proj(p2, wg3, c * 128, 128, h, T)
                k.tt(u_ext.cs(c, (slice(None), slice(2, 2 + T))), p2, acs, ALU.mult)
                pcv = k.ps()
                for tap in range(3):
                    k.mm(pcv, dsc[:, tap * 2 + c, :], u_ext.cs(c, (slice(None), slice(tap, tap + T))), start=(tap == 0), stop=(tap == 2))
                p3 = k.ps()
                k.proj(p3, wg2, c * 128, 128, h, T)
                k.act(abs_, p3, AF.Copy)
                k.tt(ya.c(c), pcv, abs_, ALU.mult)
                k.copy(u_ext.cs(c, (slice(None), slice(0, 2))), u_ext.cs(c, (slice(None), slice(T, T + 2))), eng="pool")
            for c in range(2):
                k.copy(qtb.c(c), qtil.c(c))
                p = k.ps()
                k.proj(p, wg3, 256 + c * 128, 128, h, T)
                k.act(sr, p, AF.Silu)
                for hh_ in range(2):
                    hr = hh_ * 64
                    pc_ = k.ps()
                    k.mm(pc_[hr:hr + 64, :], Sib[hr:hr + 64, c, hr:hr + 64], qtb.cs(c, (slice(hr, hr + 64), slice(None))),
                         start=True, stop=True, tp=(hr, hr))
                    k.tt(o32[hr:hr + 64, :], pc_[hr:hr + 64, :], opart.cs(c, (slice(hr, hr + 64), slice(None))), ALU.add)
                k.act(sqb, o32, AF.Square)
                pss = k.ps()
                k.mm(pss, onesbd, sqb)
                k.act(rs, pss, AF.Sqrt, bias=EPS, scale=1.0 / 64.0)
                k.recip(rs, rs)
                k.stt(t1, o32, cv[:, C_GNG + c:C_GNG + c + 1], rs, ALU.mult, ALU.mult)
                k.tt(yb.c(c), t1, sr, ALU.mult)
            wg4 = k.wload(w_in, KC, 512, 0, WI_SU)
            for c in range(2):
                p = k.ps()
                k.proj(p, wg4, c * 128, 128, h, T)
                k.act(sus.c(c), p, AF.Copy)
            pmix = [k.ps(hold=True), k.ps(hold=True)]
            for b in range(4):
                p = k.ps()
                for kc in range(KC):
                    k.mm(p[:, 0:256], h.cs(kc, (slice(None), slice(b * 128, (b + 1) * 128))), wg4[:, kc, 256:512],
                         start=(kc == 0), stop=(kc == KC - 1))
                k.memset(st4, 0.0, eng="dve")
                k.act(sv32, p[:, 0:256], AF.Identity, accum=st4[:, 0:1])
                k.ts(st4[:, 1:2], st4[:, 0:1], -1.0 / 256.0, None, ALU.mult)
                k.act(junk, sv32, AF.Square, bias=st4[:, 1:2], accum=st4[:, 2:3])
                k.act(st4[:, 3:4], st4[:, 2:3], AF.Sqrt, bias=EPS, scale=1.0 / 256.0)
                k.recip(st4[:, 3:4], st4[:, 3:4])
                k.ts(vn, sv32, st4[:, 1:2], st4[:, 3:4], ALU.add, ALU.mult)
                k.tt(vn, vn, lnG, ALU.mult)
                vb = vnb[b % 2]
                k.tt(vb, vn, lnB, ALU.add)
                for g in range(4):
                    c, hr = g // 2, (g % 2) * 64
                    k.mm(pmix[c][hr:hr + 64, b * 128:(b + 1) * 128], vb[:, g * 64:(g + 1) * 64], wm[:, g, :],
                         start=True, stop=True, tp=(0, hr))
            for c in range(2):
                k.tt(Tl(m1.ap.rearrange("p (b i) -> p b i", i=128), m1.bufs),
                     Tl(pmix[c].ap.rearrange("p (b i) -> p b i", i=128), pmix[c].bufs),
                     Tl(bsT.ap[:, c, :].unsqueeze(1).to_broadcast([128, 4, 128]), bsT.bufs), ALU.add)
                k.tt(yc.c(c), sus.c(c), m1, ALU.mult)
                k.release(pmix[c])
            wg5 = k.wload(w_in, KC, 256, 0, WI_RG)
            for c in range(2):
                p = k.ps()
                k.proj(p, wg5, c * 128, 128, h, T)
                k.act(gg, p, AF.Gelu)
                k.stt(hf, pcum.c(c), hi[:, c:c + 1], hloc.c(c), ALU.mult, ALU.add)
                k.tt(yd.c(c), hf, gg, ALU.mult)
            ys = [ya, yb, yc, yd]
            wbr = None
            for m in range(KC):
                wbr = k.wload(w_br, 8, 128, 0, m * 128)
                wgt = k.wload(w_gate, KC, 512, 0, m * 512)
                col = 0
                mg = k.ps(hold=True)
                for kb in range(4):
                    pb = k.ps()
                    for kc in range(2):
                        k.mm(pb, wbr[:, kb * 2 + kc, col:col + 128], ys[kb].c(kc), start=(kc == 0), stop=(kc == 1))
                    pg = k.ps()
                    k.proj(pg, wgt, kb * 128, 128, h, T)
                    gt = gts[kb % 2]
                    pr = prods[kb % 2]
                    k.act(gt, pg, AF.Sigmoid, bias=cv[:, C_BG + kb * 8 + m:C_BG + kb * 8 + m + 1])
                    k.tt(pr, pb, gt, ALU.mult)
                    k.mm(mg, k.ident, pr, start=(kb == 0), stop=(kb == 3))
                k.act(merged.c(m), mg, AF.Copy)
                k.release(mg)
            wmx = None
            for m in range(KC):
                if m % 4 == 0:
                    wmx = k.wload(w_mix, KC, 512, 0, (m // 4) * 512)
                p = k.ps()
                k.proj(p, wmx, (m % 4) * 128, 128, merged, T)
                k.act(ybuf.c(m), p, AF.Copy)
            k.postnorm_residual(xs, ybuf, C_G + 1 * 8, cv, sq, rstd, tmp)
            for _xa in ([] if SKIP_XA[0] else [0]):
                k.rmsnorm_h(h, xs, T, C_G + 2 * 8, cv, sq, rstd)
                wqs = None
                for m in range(KC):
                    if m % 4 == 0:
                        wqs = k.wload(w_q, KC, 512, 0, (m // 4) * 512)
                    p = k.ps()
                    k.proj(p, wqs, (m % 4) * 128, 128, h, T)
                    k.act(qT.c(m), p, AF.Copy)
                for hd in range(4):
                    for mc in range(2):
                        p = k.ps()
                        for dc in range(2):
                            k.mm(p, KT.cs(2 * hd + dc, (slice(None), slice(mc * 128, (mc + 1) * 128))), qT.c(2 * hd + dc),
                                 start=(dc == 0), stop=(dc == 1))
                        k.act(E.c(mc), p, AF.Exp, scale=1.0 / 16.0)
                    pden = k.ps()
                    for mc in range(2):
                        k.mm(pden, k.ones, E.c(mc), start=(mc == 0), stop=(mc == 1))
                    k.recip(rden, pden)
                    for dc in range(2):
                        po = k.ps()
                        for mc in range(2):
                            k.mm(po, Vt.cs(mc, (slice(None), slice((2 * hd + dc) * 128, (2 * hd + dc + 1) * 128))), E.c(mc),
                                 start=(mc == 0), stop=(mc == 1))
                        k.tt(oT.c(2 * hd + dc), po, rden, ALU.mult)
                wos = None
                for m in range(KC):
                    if m % 4 == 0:
                        wos = k.wload(w_o, KC, 512, 0, (m // 4) * 512)
                    p = k.ps()
                    k.proj(p, wos, (m % 4) * 128, 128, oT, T)
                    k.act(ybuf.c(m), p, AF.Copy)
                k.postnorm_residual(xs, ybuf, C_G + 3 * 8, cv, sq, rstd, tmp)
            k.dma(xo[:, :, t0:t0 + T], x, sem="st_x")
        k.finish([xo])
    return nc


def build_C(NT):
    NTOK = NT * T
    nc = bass.Bass("TRN2", target_bir_lowering=False)
    with ExitStack() as st:
        k = KB(nc, st)
        xT = k.dram_in("xT", [128, KC, NTOK])
        xh = k.dram_in("xh", [128, KC, 4])
        cvd = k.dram_in("cvec", [128, NV])
        w_up = k.dram_in("w_up", [128, KC, 2 * DFF])
        dffd = k.dram_in("diag_ffn", [128, NFC * 3, 128])
        w_dn = k.dram_in("w_dn", [128, 22, 1024])
        xo = k.dram_out("xo", [128, KC, NTOK])
        k.init_psum()
        k.consts()
        k.init_ring(5)
        cv = k.sb([128, NV], F32); k.dma(cv, cvd)
        x = k.sb([128, KC, T], F32, nchunk=KC)
        h = k.sb([128, KC, T], BF16, nchunk=KC)
        sq = k.sb([128, KC, T], BF16, nchunk=KC)
        rstd = k.sb([128, T], F32)
        tmp = k.sb([128, T], F32)
        ybuf = k.sb([128, KC, T], F32, nchunk=KC)
        xhs = k.sb([128, KC, 4], F32, nchunk=KC)
        hh = k.sb([128, KC, 4], BF16, nchunk=KC)
        hal = [k.sb([128, NFC, 2], BF16) for _ in range(2)]
        us = [k.sb([128, T], BF16) for _ in range(3)]
        gls = [k.sb([128, T], F32) for _ in range(2)]
        hmid = k.sb([128, 22, T], BF16, nchunk=22)
        xs = [x.c(kc) for kc in range(KC)]

        k.dma(xhs, xh)
        k.rmsnorm_h(hh, [xhs.c(kc) for kc in range(KC)], 4, C_G + 5 * 8, cv, sq, rstd)
        for g in range(11):
            wup = k.wload(w_up, KC, 512, 0, g * 512)
            for j in range(4):
                ch = g * 4 + j
                p = k.ps()
                k.proj(p, wup, j * 128, 128, hh, 4)
                k.act(hal[0][:, ch, :], p[:, 2:4], AF.Copy)

        ui = 0
        for it in range(NT):
            t0 = it * T
            cur, nxt = hal[it % 2], hal[(it + 1) % 2]
            k.dma(x, xT[:, :, t0:t0 + T], sem="xld")
            k.rmsnorm_h(h, xs, T, C_G + 5 * 8, cv, sq, rstd)
            for g in range(11):
                wup = k.wload(w_up, KC, 512, 0, g * 512)
                dff = k.wload(dffd, 12, 128, g * 12, 0)
                for j in range(4):
                    ch = g * 4 + j
                    p = k.ps()
                    k.proj(p, wup, j * 128, 128, h, T)
                    u = us[ui % 3]
                    ui += 1
                    k.act(u, p, AF.Copy)
                    k.act(nxt[:, ch, :], p[:, T - 2:T], AF.Copy)
                    pc_ = k.ps()
                    d0, d1, d2 = dff[:, j * 3 + 0, :], dff[:, j * 3 + 1, :], dff[:, j * 3 + 2, :]
                    k.mm(pc_[:, 0:T], d2, u[:, 0:T], start=True, stop=False)
                    k.mm(pc_[:, 1:T], d1, u[:, 0:T - 1], start=False, stop=False)
                    k.mm(pc_[:, 0:1], d1, cur[:, ch, 1:2], start=False, stop=False)
                    k.mm(pc_[:, 2:T], d0, u[:, 0:T - 2], start=False, stop=False)
                    k.mm(pc_[:, 0:2], d0, cur[:, ch, 0:2], start=False, stop=True)
                    fcb = cv[:, C_FCB + ch:C_FCB + ch + 1]
                    if j % 2 == 0:
                        k.act(gls[(ch // 2) % 2], pc_, AF.Gelu, bias=fcb)
                    else:
                        k.stt(hmid.c(ch // 2), pc_, fcb, gls[(ch // 2) % 2], ALU.add, ALU.mult)
            for cg in range(2):
                accs = [k.ps(hold=True) for _ in range(4)]
                for (r0, rows) in ((0, 8), (8, 8), (16, 6)):
                    wd = k.wload(w_dn, rows, 512, r0, cg * 512)
                    for mi in range(4):
                        for r in range(rows):
                            k.mm(accs[mi], wd[:, r, mi * 128:(mi + 1) * 128], hmid.c(r0 + r), start=(r0 + r == 0), stop=(r0 + r == 21))
                for mi in range(4):
                    k.act(ybuf.c(cg * 4 + mi), accs[mi], AF.Copy)
                    k.release(accs[mi])
            k.postnorm_residual(xs, ybuf, C_G + 6 * 8, cv, sq, rstd, tmp)
            k.dma(xo[:, :, t0:t0 + T], x, sem="st_x")
        k.finish([xo])
    return nc


def _fm(w):
    K, N = w.shape
    return np.ascontiguousarray(w.reshape(K // 128, 128, N).transpose(1, 0, 2))


def _cv(v):
    return np.asarray(v).reshape(-1, 128).T


def _tok_fm(a):
    n = a.shape[0]
    return np.ascontiguousarray(a.T.reshape(KC, 128, n).transpose(1, 0, 2))


def _diag(v):
    m = np.zeros((128, 128), np.float32)
    m[np.arange(128), np.arange(128)] = v
    return m


_PERM = np.concatenate([np.concatenate([np.arange(q * 128, (q + 1) * 128), DFF + np.arange(q * 128, (q + 1) * 128)])
                        for q in range(22)])


def prep_layer(inp, l):
    f = lambda k_: np.asarray(inp[k_], dtype=np.float32)[l]
    cv = np.zeros((128, NV), np.float32)
    ng = f("norm_g")
    for i in range(7):
        cv[:, C_G + i * 8:C_G + (i + 1) * 8] = _cv(ng[i])
    cv[:, C_GNG:C_GNG + 2] = _cv(f("gla_norm_g"))
    cv[:, C_LCB:C_LCB + 2] = _cv(f("lru_conv_b"))
    cv[:, C_LBA:C_LBA + 2] = _cv(f("lru_b_a"))
    cv[:, C_LBX:C_LBX + 2] = _cv(f("lru_b_x"))
    cv[:, C_LAM:C_LAM + 2] = _cv(f("lru_lambda"))
    bg = f("b_gate")
    for kb in range(4):
        cv[:, C_BG + kb * 8:C_BG + (kb + 1) * 8] = _cv(bg[kb])
    cv[:, C_FCB:C_FCB + NFC] = _cv(f("ffn_conv_b")[_PERM])
    wi = f("w_in")
    sp = {"a_b": (0, 256), "a_c": (256, 512), "a_x": (512, 768), "q": (768, 1024), "k": (1024, 1280), "v": (1280, 1536),
          "r": (1536, 1792), "a_lr": (1792, 1808), "su": (1808, 2064), "sv": (2064, 2320), "rx": (2320, 2576), "rg": (2576, 2832)}
    order = ["q", "rx", "k", "v", "a_b", "a_c", "a_x", "r", "su", "sv", "rg", "a_lr"]
    w_in_r = np.concatenate([wi[:, sp[n][0]:sp[n][1]] for n in order], axis=1)
    out = {"cvec": cv, "w_in": _fm(w_in_r)}
    out["w_gate"] = _fm(np.ascontiguousarray(f("w_gate").reshape(D, 4, 8, 128).transpose(0, 2, 1, 3)).reshape(D, 4096))
    out["w_br"] = np.ascontiguousarray(f("w_branch").reshape(4, 2, 128, D).transpose(2, 0, 1, 3)).reshape(128, 8, D)
    out["w_mix"] = _fm(f("w_mix_out"))
    out["w_q"] = _fm(f("xa_wq"))
    out["w_kv"] = _fm(f("xa_wkv"))
    out["w_o"] = _fm(f("xa_wo"))
    out["w_up"] = _fm(np.ascontiguousarray(f("ffn_w_up")[:, _PERM]))
    out["w_dn"] = _fm(f("ffn_w_down"))
    lcw = f("lru_conv_w")
    out["diag_lru"] = np.stack([_diag(lcw[tap, c * 128:(c + 1) * 128]) for tap in range(4) for c in range(2)], axis=1)
    scw = f("sc_conv_w")
    out["diag_sc"] = np.stack([_diag(scw[tap, c * 128:(c + 1) * 128]) for tap in range(3) for c in range(2)], axis=1)
    fcw = f("ffn_conv_w")[:, _PERM]
    out["diag_ffn"] = np.stack([_diag(fcw[tap, ch * 128:(ch + 1) * 128]) for ch in range(NFC) for tap in range(3)], axis=1)
    for nm, key in (("bd_a", "lru_w_a"), ("bd_x", "lru_w_x")):
        w = f(key)
        bd = np.zeros((128, 2, 128), np.float32)
        for c in range(2):
            bd[0:64, c, 0:64] = w[2 * c]
            bd[64:128, c, 64:128] = w[2 * c + 1]
        out[nm] = bd
    out["WmT"] = np.ascontiguousarray(f("sgu_w").transpose(2, 0, 1))
    out["lnG"] = np.ascontiguousarray(np.broadcast_to(f("sgu_ln_g")[None, :], (128, 256)))
    out["lnB"] = np.ascontiguousarray(np.broadcast_to(f("sgu_ln_b")[None, :], (128, 256)))
    sb_ = f("sgu_b")
    bsT = np.zeros((128, 2, 128), np.float32)
    for c in range(2):
        bsT[0:64, c, :] = sb_[2 * c][None, :]
        bsT[64:128, c, :] = sb_[2 * c + 1][None, :]
    out["bsT"] = bsT
    out["walpha"] = np.ascontiguousarray(f("gla_w_alpha"))
    out["balpha"] = np.ascontiguousarray(f("gla_b_alpha")[None, :])
    return {k_: np.ascontiguousarray(v, dtype=np.float32) for k_, v in out.items()}


_PROGS = {}


def _prog(name, NT):
    key = (name, NT)
    if key not in _PROGS:
        _PROGS[key] = {"A": build_A, "B": build_B, "C": build_C}[name](NT)
    return _PROGS[key]


def _halo(xcores, NC_PER):
    out = []
    for c in range(len(xcores)):
        if c % NC_PER == 0:
            out.append(np.zeros((128, KC, 4), np.float32))
        else:
            out.append(np.ascontiguousarray(xcores[c - 1][:, :, -4:]))
    return out


A_KEYS = ("cvec", "w_in", "diag_lru", "bd_a", "bd_x", "walpha", "balpha")
B_KEYS = ("cvec", "w_in", "w_gate", "w_br", "w_mix", "w_q", "w_kv", "w_o", "diag_sc", "WmT", "lnG", "lnB", "bsT")
C_KEYS = ("cvec", "w_up", "diag_ffn", "w_dn")


def kernel(**inputs):
    x = np.asarray(inputs["x"], dtype=np.float32)
    mem = np.asarray(inputs["mem"], dtype=np.float32)
    B_, S_, _ = x.shape
    NCORE = 8
    NC_PER = NCORE // B_
    NTOK = S_ // NC_PER
    NT = NTOK // T
    cores = list(range(NCORE))
    xc = [_tok_fm(x[c // NC_PER, (c % NC_PER) * NTOK:(c % NC_PER + 1) * NTOK]) for c in cores]
    memT = [_tok_fm(mem[b]) for b in range(B_)]
    sels = []
    for c in cores:
        r, gb = c % NC_PER, (c // NC_PER) * NC_PER
        s = np.zeros((3, 8), np.float32)
        for j in range(3):
            src = r - 3 + j
            if src >= 0:
                s[j, gb + src] = 1.0
        sels.append(np.ascontiguousarray(np.broadcast_to(s.reshape(1, 24), (128, 24))))
    depth = np.asarray(inputs["norm_g"]).shape[0]
    for l in range(depth):
        P = prep_layer(inputs, l)
        xh = _halo(xc, NC_PER)
        resA = run_bass_kernel_spmd(_prog("A", NT), [dict({k_: P[k_] for k_ in A_KEYS}, xT=xc[c], xh=xh[c]) for c in cores],
                                    core_ids=cores).results
        links = np.ascontiguousarray(np.stack([np.asarray(resA[c]["link"]) for c in cores], axis=1))
        resB = run_bass_kernel_spmd(
            _prog("B", NT),
            [dict({k_: P[k_] for k_ in B_KEYS}, xT=xc[c], xh=xh[c], memT=memT[c // NC_PER],
                  opart=np.asarray(resA[c]["opart"]), qtil=np.asarray(resA[c]["qtil"]),
                  hloc=np.asarray(resA[c]["hloc"]), pcum=np.asarray(resA[c]["pcum"]), links=links, sel=sels[c]) for c in cores],
            core_ids=cores).results
        xc = [np.asarray(resB[c]["xo"]) for c in cores]
        xh = _halo(xc, NC_PER)
        resC = run_bass_kernel_spmd(_prog("C", NT), [dict({k_: P[k_] for k_ in C_KEYS}, xT=xc[c], xh=xh[c]) for c in cores],
                                    core_ids=cores).results
        xc = [np.asarray(resC[c]["xo"]) for c in cores]
    out = np.zeros((B_, S_, D), np.float32)
    for c in cores:
        out[c // NC_PER, (c % NC_PER) * NTOK:(c % NC_PER + 1) * NTOK] = xc[c].transpose(2, 1, 0).reshape(NTOK, D)
    return out
```

```python
import numpy as np
from contextlib import ExitStack
import concourse.bass as bass
import concourse.mybir as mybir
from concourse.bass_utils import run_bass_kernel_spmd

F32 = mybir.dt.float32
BF16 = mybir.dt.bfloat16
ALU = mybir.AluOpType
AF = mybir.ActivationFunctionType

D = 1024
KC = 8
T = 512
NB = 4
EPS = 1e-6
DFF = 2816
NFC = 44
NLINK = 2 * 128 + 6

C_G = 0
C_GNG = 56
C_LCB = 58
C_LBA = 60
C_LBX = 62
C_LAM = 64
C_BG = 66
C_FCB = 98
NV = 142

WI_Q, WI_RX, WI_K, WI_V, WI_AB, WI_AC, WI_AX, WI_R, WI_SU, WI_SV, WI_RG, WI_ALR = (
    0, 256, 512, 768, 1024, 1280, 1536, 1792, 2048, 2304, 2560, 2816)


class Buf:
    __slots__ = ("w", "r", "excl")

    def __init__(self, excl=False):
        self.w = None
        self.r = {}
        self.excl = excl


class Tl:
    __slots__ = ("ap", "bufs", "cb")

    def __init__(self, ap, bufs, cb=None):
        self.ap = ap
        self.bufs = tuple(bufs)
        self.cb = cb

    def __getitem__(self, key):
        return Tl(self.ap[key], self.bufs)

    def _cb(self, i):
        b = self.cb[i]
        return b if isinstance(b, tuple) else (b,)

    def c(self, i):
        return Tl(self.ap[:, i], self._cb(i))

    def cs(self, i, key):
        return Tl(self.ap[:, i][key], self._cb(i))


class Sched:
    ENG = ("pe", "act", "dve", "pool", "sp")

    def __init__(self, nc, stack):
        self.nc = nc
        self.stack = stack
        self.sems = {}
        self.cnt = {}
        self.waited = {k: {} for k in self.ENG}
        self.prog = {k: [] for k in self.ENG}
        self.ptoken = Buf(excl=True)
        for k in self.ENG:
            self.sems[k] = stack.enter_context(nc.semaphore("s_" + k))
            self.cnt[k] = 0

    def _deps(self, eng, reads, writes):
        deps = {}

        def add(ev, same_ok):
            if ev is None:
                return
            k, v = ev
            if k == eng and same_ok:
                return
            if deps.get(k, 0) < v:
                deps[k] = v

        for b in reads:
            add(b.w, eng == "pe")
            if b.excl:
                for k, v in b.r.items():
                    add((k, v), True)
        for b in writes:
            add(b.w, True)
            for k, v in b.r.items():
                add((k, v), True)
        out = []
        wd = self.waited[eng]
        for k, v in deps.items():
            if wd.get(k, 0) < v:
                wd[k] = v
                out.append((k, v))
        return out

    def _commit(self, ev, reads, writes):
        k, v = ev
        for b in writes:
            b.w = ev
            b.r = {}
        for b in reads:
            if b.r.get(k, 0) < v:
                b.r[k] = v

    def op(self, eng, fn, reads=(), writes=()):
        if PSUM_LOCK[0] and eng in ("act", "dve") and any(b.excl for b in reads):
            reads = list(reads) + [self.ptoken]
        waits = self._deps(eng, reads, writes)
        self.cnt[eng] += 1
        self.prog[eng].append((waits, fn, eng, 1))
        self._commit((eng, self.cnt[eng]), reads, writes)

    def dma(self, q, out_ap, in_ap, reads=(), writes=(), sem=None):
        if sem not in self.sems:
            self.sems[sem] = self.stack.enter_context(self.nc.semaphore("d_" + sem))
            self.cnt[sem] = 0
        waits = self._deps(q, reads, writes)
        self.cnt[sem] += 16
        self.prog[q].append((waits, lambda e: e.dma_start(out=out_ap, in_=in_ap), sem, 16))
        self._commit((sem, self.cnt[sem]), reads, writes)

    def custom(self, q, fn, reads=(), writes=(), sem=None, inc=1):
        if sem not in self.sems:
            self.sems[sem] = self.stack.enter_context(self.nc.semaphore("d_" + sem))
            self.cnt[sem] = 0
        waits = self._deps(q, reads, writes)
        self.cnt[sem] += inc
        self.prog[q].append((waits, fn, sem, inc))
        self._commit((sem, self.cnt[sem]), reads, writes)

    def barrier(self):
        for eng in self.ENG:
            waits = []
            wd = self.waited[eng]
            for k, v in self.cnt.items():
                if k != eng and v > 0 and wd.get(k, 0) < v:
                    wd[k] = v
                    waits.append((k, v))
            if waits:
                self.prog[eng].append((waits, None, None, 0))

    def flush(self):
        with self.nc.Block() as block:
            self.emit(block)
        self.prog = {k: [] for k in self.ENG}

    def wait_all(self, eng, bufs):
        waits = self._deps(eng, bufs, ())
        self.prog[eng].append((waits, None, None, 0))

    def emit(self, block):
        sems = self.sems
        prog = self.prog

        def run(e, lst):
            for waits, fn, semk, inc in lst:
                for k, v in waits:
                    e.wait_ge(sems[k], v)
                if fn is not None:
                    fn(e).then_inc(sems[semk], inc)

        @block.tensor
        def _(e):
            run(e, prog["pe"])

        @block.scalar
        def _(e):
            run(e, prog["act"])

        @block.vector
        def _(e):
            run(e, prog["dve"])

        @block.gpsimd
        def _(e):
            run(e, prog["pool"])

        @block.sync
        def _(e):
            run(e, prog["sp"])


class StopBuild(Exception):
    pass


STOP = [99]
PSUM_LOCK = [True]
PIPE = {'merge': True, 'xa': True, 'conv': True}
SKIP_XA = [False]


class KB:
    def __init__(self, nc, st):
        self.nc = nc
        self.st = st
        self.S = Sched(nc, st)
        self.uid = 0
        self.psb = []
        self.psi = 0
        self.held = set()
        self.wsem = 0
        self.cur = None
        self.lyr = None
        self.wcache = {}
        self.ncache = 0

    def ck(self, n):
        if n >= STOP[0]:
            raise StopBuild()

    def name(self, p):
        self.uid += 1
        return "%s%d" % (p, self.uid)

    def sb(self, shape, dt, nchunk=None):
        t = (self.cur or self.st).enter_context(self.nc.sbuf_tensor(self.name("t"), list(shape), dt))
        if nchunk:
            cb = [Buf() for _ in range(nchunk)]
            return Tl(t[:], cb, cb)
        return Tl(t[:], [Buf()])

    def dram_in(self, name, shape, dt=F32):
        return Tl(self.nc.dram_tensor(name, list(shape), dt, kind="ExternalInput").ap(), [Buf()])

    def dram_out(self, name, shape, dt=F32):
        return Tl(self.nc.dram_tensor(name, list(shape), dt, kind="ExternalOutput").ap(), [Buf()])

    def init_psum(self):
        for i in range(7):
            t = self.st.enter_context(self.nc.psum_tensor(self.name("ps"), [128, 512], F32))
            self.psb.append(Tl(t[:], [Buf(excl=True)]))
        t = self.st.enter_context(self.nc.psum_tensor(self.name("pst"), [128, 1024], BF16))
        self.pst = Tl(t[:], [Buf(excl=True)])

    def ps(self, hold=False):
        for _ in range(8):
            i = self.psi
            self.psi = (self.psi + 1) % 7
            if i not in self.held:
                if hold:
                    self.held.add(i)
                return self.psb[i]
        raise RuntimeError("no psum")

    def release(self, t):
        for i, p in enumerate(self.psb):
            if p.bufs[0] is t.bufs[0]:
                self.held.discard(i)

    @staticmethod
    def _rb(*ts):
        out = []
        for t in ts:
            if isinstance(t, Tl):
                out.extend(t.bufs)
        return out

    def mm(self, out, lhsT, rhs, start=True, stop=True, tp=None):
        if tp is None:
            fn = lambda e: e.matmul(out.ap, lhsT=lhsT.ap, rhs=rhs.ap, start=start, stop=stop)
        else:
            fn = lambda e: e.matmul(out.ap, lhsT=lhsT.ap, rhs=rhs.ap, start=start, stop=stop, tile_position=tp)
        self.S.op("pe", fn, reads=self._rb(lhsT, rhs), writes=self._rb(out))

    def transpose(self, out, in_, ident):
        self.S.op("pe", lambda e: e.transpose(out.ap, in_.ap, ident.ap),
                  reads=self._rb(in_, ident), writes=self._rb(out))

    def act(self, out, in_, func, bias=None, scale=None, accum=None):
        kw = {}
        if bias is not None:
            kw["bias"] = bias.ap if isinstance(bias, Tl) else bias
        if scale is not None:
            kw["scale"] = scale.ap if isinstance(scale, Tl) else scale
        if accum is not None:
            kw["accum_out"] = accum.ap
        if func == AF.Copy and (isinstance(bias, Tl) or isinstance(scale, Tl)):
            func = AF.Identity
        self.S.op("act", lambda e: e.activation(out=out.ap, in_=in_.ap, func=func, **kw),
                  reads=self._rb(in_, bias, scale), writes=self._rb(out, accum))

    def tt(self, out, a, b, op, eng="dve"):
        self.S.op(eng, lambda e: e.tensor_tensor(out=out.ap, in0=a.ap, in1=b.ap, op=op),
                  reads=self._rb(a, b), writes=self._rb(out))

    def stt(self, out, in0, scalar, in1, op0, op1):
        sc = scalar.ap if isinstance(scalar, Tl) else scalar
        self.S.op("dve", lambda e: e.scalar_tensor_tensor(out=out.ap, in0=in0.ap, scalar=sc, in1=in1.ap, op0=op0, op1=op1),
                  reads=self._rb(in0, scalar, in1), writes=self._rb(out))

    def ts(self, out, in0, s1, s2, op0, op1=None, eng="dve"):
        a1 = s1.ap if isinstance(s1, Tl) else s1
        a2 = s2.ap if isinstance(s2, Tl) else s2
        if op1 is None:
            fn = lambda e: e.tensor_scalar(out=out.ap, in0=in0.ap, scalar1=a1, scalar2=None, op0=op0)
        else:
            fn = lambda e: e.tensor_scalar(out=out.ap, in0=in0.ap, scalar1=a1, scalar2=a2, op0=op0, op1=op1)
        self.S.op(eng, fn, reads=self._rb(in0, s1, s2), writes=self._rb(out))

    def copy(self, out, in_, eng="dve"):
        self.S.op(eng, lambda e: e.tensor_copy(out=out.ap, in_=in_.ap), reads=self._rb(in_), writes=self._rb(out))

    def recip(self, out, in_):
        self.S.op("dve", lambda e: e.reciprocal(out=out.ap, in_=in_.ap), reads=self._rb(in_), writes=self._rb(out))

    def scan(self, out, d0, d1, init, op0, op1):
        ini = init.ap if isinstance(init, Tl) else init
        self.S.op("dve", lambda e: e.tensor_tensor_scan(out=out.ap, data0=d0.ap, data1=d1.ap, initial=ini, op0=op0, op1=op1),
                  reads=self._rb(d0, d1, init), writes=self._rb(out))

    def memset(self, out, val, eng="pool"):
        self.S.op(eng, lambda e: e.memset(out.ap, val), writes=self._rb(out))

    def aselect(self, t, pattern, cmp, base, cm):
        self.S.op("pool", lambda e: e.affine_select(out=t.ap, in_=t.ap, pattern=pattern, compare_op=cmp, fill=0.0,
                                                    base=base, channel_multiplier=cm),
                  reads=self._rb(t), writes=self._rb(t))

    def dma(self, out, in_, q="sp", sem=None):
        if sem is None:
            self.wsem += 1
            sem = "x%d" % self.wsem
        self.S.dma(q, out.ap, in_.ap, reads=self._rb(in_), writes=self._rb(out), sem=sem)

    def consts(self):
        self.identf = self.sb([128, 128], F32)
        self.memset(self.identf, 1.0)
        self.aselect(self.identf, [[-1, 128]], ALU.is_equal, 0, 1)
        self.ident = self.sb([128, 128], BF16)
        self.copy(self.ident, self.identf)
        self.ones = self.sb([128, 128], BF16)
        self.memset(self.ones, 1.0)
        self.zeros = self.sb([128, 512], F32)
        self.memset(self.zeros, 0.0)

    def init_ring(self, nslots=4):
        self.ring = [self.sb([128, 4096], BF16) for _ in range(nslots)]
        self.ri = 0

    def wload(self, dram, rows, cols, r0=0, c0=0, cache=None):
        if self.lyr is not None and len(dram.ap.shape) == 4:
            dram = Tl(dram.ap[self.lyr], dram.bufs)
        slot = self.ring[self.ri]
        i = self.ri
        self.ri = (self.ri + 1) % len(self.ring)
        view = Tl(slot.ap[:, 0:rows * cols].rearrange("p (r c) -> p r c", c=cols), slot.bufs)
        if cache is not None:
            ent = self.wcache.get(cache)
            flat = Tl(slot.ap[:, 0:rows * cols], slot.bufs)
            if ent is not None and ent[1]:
                self.S.dma("sp", flat.ap, ent[0].ap[:, 0:rows * cols], reads=self._rb(ent[0]), writes=self._rb(flat),
                           sem="ringh%d" % i)
                return view
            if ent is None:
                self.ncache += 1
                t = self.nc.dram_tensor("wc%d" % self.ncache, [128, 4096], BF16)
                ent = [Tl(t.ap(), [Buf()]), False]
                self.wcache[cache] = ent
            self.S.dma("pool", view.ap, dram.ap[:, r0:r0 + rows, c0:c0 + cols], reads=self._rb(dram), writes=self._rb(view),
                       sem="ring%d" % i)
            self.S.dma("sp", ent[0].ap[:, 0:rows * cols], flat.ap, reads=self._rb(flat), writes=self._rb(ent[0]),
                       sem="wcst%d" % (self.ncache % 4))
            ent[1] = True
            return view
        self.S.dma("pool", view.ap, dram.ap[:, r0:r0 + rows, c0:c0 + cols], reads=self._rb(dram), writes=self._rb(view),
                   sem="ring%d" % i)
        return view

    def rmsnorm_h(self, h, xs, n, gcol, cv, sq, rstd_t):
        p = self.ps()
        for kc in range(KC):
            self.act(sq.cs(kc, (slice(None), slice(0, n))), xs[kc], AF.Square)
        for kc in range(KC):
            self.mm(p[:, 0:n], self.ones, sq.cs(kc, (slice(None), slice(0, n))), start=(kc == 0), stop=(kc == KC - 1))
        self.act(rstd_t[:, 0:n], p[:, 0:n], AF.Sqrt, bias=EPS, scale=1.0 / D)
        self.recip(rstd_t[:, 0:n], rstd_t[:, 0:n])
        for kc in range(KC):
            self.stt(h.cs(kc, (slice(None), slice(0, n))), xs[kc], cv[:, gcol + kc:gcol + kc + 1], rstd_t[:, 0:n], ALU.mult, ALU.mult)

    def proj(self, p, w, c0, m, h, n):
        for kc in range(KC):
            self.mm(p[0:m, 0:n], w[:, kc, c0:c0 + m], h.cs(kc, (slice(None), slice(0, n))), start=(kc == 0), stop=(kc == KC - 1))

    def postnorm_residual(self, xs, ybuf, gcol, cv, sq, rstd_t, tmp):
        p = self.ps()
        for kc in range(KC):
            self.act(sq.c(kc), ybuf.c(kc), AF.Square)
        for kc in range(KC):
            self.mm(p, self.ones, sq.c(kc), start=(kc == 0), stop=(kc == KC - 1))
        self.act(rstd_t, p, AF.Sqrt, bias=EPS, scale=1.0 / D)
        self.recip(rstd_t, rstd_t)
        for kc in range(KC):
            self.stt(tmp, ybuf.c(kc), cv[:, gcol + kc:gcol + kc + 1], rstd_t, ALU.mult, ALU.mult)
            self.tt(xs[kc], xs[kc], tmp, ALU.add)

    def finish(self, outs):
        bufs = []
        for o in outs:
            bufs.extend(o.bufs)
        self.S.wait_all("sp", bufs)
        with self.nc.Block() as block:
            self.S.emit(block)


def build_A(NT):
    NTOK = NT * T
    nc = bass.Bass("TRN2", target_bir_lowering=False)
    with ExitStack() as st:
        k = KB(nc, st)
        xT = k.dram_in("xT", [128, KC, NTOK])
        xh = k.dram_in("xh", [128, KC, 4])
        cvd = k.dram_in("cvec", [128, NV])
        w_in = k.dram_in("w_in", [128, KC, 2832])
        dlru = k.dram_in("diag_lru", [128, 8, 128])
        bda = k.dram_in("bd_a", [128, 2, 128])
        bdx = k.dram_in("bd_x", [128, 2, 128])
        wal = k.dram_in("walpha", [16, 256])
        bal = k.dram_in("balpha", [1, 256])
        o_opart = k.dram_out("opart", [128, 2, NTOK])
        o_qtil = k.dram_out("qtil", [128, 2, NTOK])
        o_hloc = k.dram_out("hloc", [128, 2, NTOK])
        o_pcum = k.dram_out("pcum", [128, 2, NTOK])
        o_link = k.dram_out("link", [128, NLINK])

        try:
            k.init_psum()
            k.consts()
            k.init_ring(4)
            cv = k.sb([128, NV], F32)
            k.dma(cv, cvd)
            dl = k.sb([128, 8, 128], BF16); k.dma(dl, dlru, q="pool")
            wa = k.sb([128, 2, 128], BF16); k.dma(wa, bda, q="pool")
            wx = k.sb([128, 2, 128], BF16); k.dma(wx, bdx, q="pool")
            walb = k.sb([16, 256], BF16); k.dma(walb, wal, q="pool")
            balb = k.sb([1, 256], BF16); k.dma(balb, bal, q="pool")
            trib = k.sb([128, 128], F32)
            k.memset(trib, 1.0)
            k.aselect(trib, [[1, 128]], ALU.is_ge, 0, -1)
            k.aselect(trib[:, 64:128], [[0, 64]], ALU.is_ge, -64, 1)
            mask4 = k.sb([128, 4, 128], BF16)
            for b in range(4):
                k.copy(mask4[:, b, :], trib)
            mask4f = Tl(mask4.ap.rearrange("p b i -> p (b i)"), mask4.bufs)
            cneg = k.sb([128, 4], F32)
            k.act(cneg[:, 0:2], cv[:, C_LAM:C_LAM + 2], AF.Exp, scale=-1.0)
            k.act(cneg[:, 0:2], cneg[:, 0:2], AF.Ln, bias=1.0)
            k.ts(cneg[:, 2:4], cneg[:, 0:2], -16.0, None, ALU.mult)
            k.ts(cneg[:, 0:2], cneg[:, 0:2], -8.0, None, ALU.mult)

            x = k.sb([128, KC, T], F32, nchunk=KC)
            h = k.sb([128, KC, T], BF16, nchunk=KC)
            sq = k.sb([128, KC, T], BF16, nchunk=KC)
            rstd = k.sb([128, T], F32)
            xhs = k.sb([128, KC, 4], F32, nchunk=KC)
            hh = k.sb([128, KC, 4], BF16, nchunk=KC)
            rx_ext = k.sb([128, 2, 3 + T], BF16, nchunk=2)
            S_st = k.sb([128, 2, 128], F32, nchunk=2)
            k.memset(S_st, 0.0)
            Dprev = k.sb([128, 2], F32); k.memset(Dprev, 1.0)
            hprev = k.sb([128, 2], F32); k.memset(hprev, 0.0)
            pprev = k.sb([128, 2], F32); k.memset(pprev, 1.0)

            k.dma(xhs, xh)
            k.rmsnorm_h(hh, [xhs.c(kc) for kc in range(KC)], 4, C_G + 0, cv, sq, rstd)
            wg0 = k.wload(w_in, KC, 512, 0, 0)
            for c in range(2):
                p = k.ps()
                k.proj(p, wg0, 256 + c * 128, 128, hh, 4)
                k.act(rx_ext.cs(c, (slice(None), slice(0, 3))), p[:, 1:4], AF.Copy)

            alr = k.sb([16, T], BF16)
            nsp = k.sb([128, 4, 256], F32, nchunk=4)
            nhi = k.sb([128, 4, 256], BF16, nchunk=4)
            nlo = k.sb([128, 4, 256], BF16, nchunk=4)
            tribb = k.sb([128, 128], BF16)
            k.copy(tribb, trib)
            ek = k.sb([128, 4, 256], F32, nchunk=4)
            kdec = k.sb([128, 4, 256], BF16, nchunk=4)
            vtok = k.sb([128, 4, 256], BF16, nchunk=4)
            kdecT = k.sb([128, 2, T], BF16, nchunk=2)
            eq = k.sb([128, 2, T], F32, nchunk=2)
            qdecT = k.sb([128, 2, T], BF16, nchunk=2)
            scT = k.sb([128, 4, T], BF16, nchunk=4)
            kvs = k.sb([128, 8, 2, 128], F32, nchunk=8)
            Sb = k.sb([128, 8, 2, 128], BF16, nchunk=8)
            opart = k.sb([128, 2, T], F32, nchunk=2)
            qtil = k.sb([128, 2, T], F32, nchunk=2)
            Dinc = k.sb([128, 2, 9], F32, nchunk=2)
            hl = k.sb([128, 2, T], F32, nchunk=2)
            pc = k.sb([128, 2, T], F32, nchunk=2)
            xc = k.sb([128, T], F32)
            xcb = k.sb([128, T], BF16)
            rg_ = k.sb([128, T], F32)
            ig_ = k.sb([128, T], F32)
            a_ = k.sb([128, T], F32)
            m_ = k.sb([128, T], F32)
            k.ck(1)
            for it in range(NT):
                t0 = it * T
                k.dma(x, xT[:, :, t0:t0 + T], sem="xld")
                k.rmsnorm_h(h, [x.c(kc) for kc in range(KC)], T, C_G + 0, cv, sq, rstd)
                if it > 0:
                    wg0 = k.wload(w_in, KC, 512, 0, 0)
                wg1 = k.wload(w_in, KC, 512, 0, 512)
                walr = k.wload(w_in, KC, 16, 0, WI_ALR)
                p = k.ps()
                k.proj(p, walr, 0, 16, h, T)
                k.act(alr, p[0:16, :], AF.Copy)
                for b2 in range(2):
                    p = k.ps()
                    for bb in range(2):
                        b = b2 * 2 + bb
                        k.mm(p[:, bb * 256:(bb + 1) * 256], alr[:, b * 128:(b + 1) * 128], walb, start=True, stop=False)
                        k.mm(p[:, bb * 256:(bb + 1) * 256], k.ones[0:1, :], balb, start=False, stop=True)
                    for bb in range(2):
                        b = b2 * 2 + bb
                        k.act(ek.c(b), p[:, bb * 256:(bb + 1) * 256], AF.Exp, scale=-1.0)
                        k.act(nsp.c(b), ek.c(b), AF.Ln, bias=1.0)
                k.ck(2)
                for b2 in range(2):
                    p = k.ps()
                    for bb in range(2):
                        b = b2 * 2 + bb
                        k.copy(nhi.c(b), nsp.c(b))
                        k.tt(nlo.c(b), nsp.c(b), nhi.c(b), ALU.subtract)
                        k.mm(p[:, bb * 256:(bb + 1) * 256], tribb, nhi.c(b), start=True, stop=False)
                        k.mm(p[:, bb * 256:(bb + 1) * 256], tribb, nlo.c(b), start=False, stop=True)
                    for bb in range(2):
                        b = b2 * 2 + bb
                        k.act(ek.c(b), p[:, bb * 256:(bb + 1) * 256], AF.Exp, scale=1.0 / 16.0)
                k.ck(2.5)
                for b in range(4):
                    p = k.ps()
                    for kc in range(KC):
                        k.mm(p, h.cs(kc, (slice(None), slice(b * 128, (b + 1) * 128))), wg1[:, kc, :], start=(kc == 0), stop=(kc == KC - 1))
                    k.ck(2.6)
                    k.tt(kdec.c(b), p[:, 0:256], ek.c(b), ALU.mult)
                    k.ck(2.7)
                    k.act(vtok.c(b), p[:, 256:512], AF.Copy)
                    k.ck(2.8)
                k.ck(3)
                for c in range(2):
                    for b in range(4):
                        k.transpose(k.pst[:, b * 128:(b + 1) * 128], kdec.cs(b, (slice(None), slice(c * 128, (c + 1) * 128))), k.ident)
                    k.copy(kdecT.c(c), k.pst[:, 0:T])
                for c in range(2):
                    p = k.ps()
                    for b in range(4):
                        k.mm(p[:, b * 128:(b + 1) * 128], nhi.cs(b, (slice(None), slice(c * 128, (c + 1) * 128))), tribb, start=True, stop=False)
                        k.mm(p[:, b * 128:(b + 1) * 128], nlo.cs(b, (slice(None), slice(c * 128, (c + 1) * 128))), tribb, start=False, stop=True)
                    k.act(eq.c(c), p, AF.Exp, scale=-1.0 / 16.0)
                for c in range(2):
                    p = k.ps()
                    k.proj(p, wg0, c * 128, 128, h, T)
                    k.stt(qdecT.c(c), p, 0.125, eq.c(c), ALU.mult, ALU.mult)
                for c in range(2):
                    p = k.ps()
                    k.proj(p, wg0, 256 + c * 128, 128, h, T)
                    k.act(rx_ext.cs(c, (slice(None), slice(3, 3 + T))), p, AF.Copy)
                k.ck(4)
                for hd in range(4):
                    c, r0 = hd // 2, (hd % 2) * 64
                    p = k.ps()
                    for b in range(4):
                        k.mm(p[:, b * 128:(b + 1) * 128], kdecT.cs(c, (slice(r0, r0 + 64), slice(b * 128, (b + 1) * 128))),
                             qdecT.cs(c, (slice(r0, r0 + 64), slice(b * 128, (b + 1) * 128))), start=True, stop=True)
                    k.tt(scT.c(hd), p, mask4f, ALU.mult)
                k.ck(4.2)
                for c in range(2):
                    for par in range(2):
                        p = k.ps()
                        r = par * 64
                        for j in range(4):
                            n = 2 * j + par
                            k.mm(p[:, j * 128:(j + 1) * 128], kdec.cs(j, (slice(r, r + 64), slice(c * 128, (c + 1) * 128))),
                                 vtok.cs(j, (slice(r, r + 64), slice(c * 128, (c + 1) * 128))), start=True, stop=True)
                        for j in range(4):
                            n = 2 * j + par
                            k.ts(Tl(kvs.ap[:, n, c, :], (kvs.cb[n],)), p[:, j * 128:(j + 1) * 128],
                                 eq.cs(c, (slice(None), slice(n * 64 + 63, n * 64 + 64))), None, ALU.mult)
                k.ck(4.5)
                for n in range(8):
                    for c in range(2):
                        k.act(Tl(Sb.ap[:, n, c, :], (Sb.cb[n],)), S_st.c(c), AF.Copy)
                        k.stt(S_st.c(c), S_st.c(c), eq.cs(c, (slice(None), slice(n * 64 + 63, n * 64 + 64))),
                              Tl(kvs.ap[:, n, c, :], (kvs.cb[n],)), ALU.mult, ALU.add)
                k.ck(5)
                for c in range(2):
                    p = k.ps()
                    for hh_ in range(2):
                        hd = c * 2 + hh_
                        hr = hh_ * 64
                        for b in range(4):
                            k.mm(p[hr:hr + 64, b * 128:(b + 1) * 128], vtok.cs(b, (slice(None), slice(hd * 64, hd * 64 + 64))),
                                 scT.cs(hd, (slice(None), slice(b * 128, (b + 1) * 128))), start=True, stop=False, tp=(0, hr))
                            for n in (2 * b, 2 * b + 1):
                                k.mm(p[hr:hr + 64, n * 64:(n + 1) * 64], Tl(Sb.ap[hr:hr + 64, n, c, hr:hr + 64], (Sb.cb[n],)),
                                     qdecT.cs(c, (slice(hr, hr + 64), slice(n * 64, (n + 1) * 64))), start=False, stop=True, tp=(hr, hr))
                    k.act(opart.c(c), p, AF.Copy)
                k.dma(o_opart[:, :, t0:t0 + T], opart, sem="st_op")
                for c in range(2):
                    k.copy(Dinc.cs(c, (slice(None), slice(0, 1))), Dprev[:, c:c + 1])
                    k.scan(Dinc.cs(c, (slice(None), slice(1, 9))), eq.cs(c, (slice(None), slice(63, T, 64))), k.zeros[:, 0:8],
                           Dprev[:, c:c + 1], ALU.mult, ALU.add)
                    k.tt(Tl(qtil.ap[:, c, :].rearrange("p (n i) -> p n i", i=64), (qtil.cb[c],)),
                         Tl(qdecT.ap[:, c, :].rearrange("p (n i) -> p n i", i=64), (qdecT.cb[c],)),
                         Tl(Dinc.ap[:, c, 0:8].unsqueeze(2).to_broadcast([128, 8, 64]), (Dinc.cb[c],)), ALU.mult)
                    k.copy(Dprev[:, c:c + 1], Dinc.cs(c, (slice(None), slice(8, 9))))
                k.dma(o_qtil[:, :, t0:t0 + T], qtil, sem="st_qt")

                k.ck(6)
                for c in range(2):
                    p = k.ps()
                    for tap in range(4):
                        k.mm(p, dl[:, tap * 2 + c, :], rx_ext.cs(c, (slice(None), slice(tap, tap + T))), start=(tap == 0), stop=(tap == 3))
                    k.act(xc, p, AF.Identity, bias=cv[:, C_LCB + c:C_LCB + c + 1])
                    k.act(xcb, p, AF.Identity, bias=cv[:, C_LCB + c:C_LCB + c + 1])
                    pr = k.ps()
                    k.mm(pr, wa[:, c, :], xcb)
                    pi = k.ps()
                    k.mm(pi, wx[:, c, :], xcb)
                    k.act(rg_, pr, AF.Sigmoid, bias=cv[:, C_LBA + c:C_LBA + c + 1])
                    k.act(ig_, pi, AF.Sigmoid, bias=cv[:, C_LBX + c:C_LBX + c + 1])
                    k.act(a_, rg_, AF.Exp, scale=cneg[:, c:c + 1])
                    k.act(m_, rg_, AF.Exp, scale=cneg[:, 2 + c:3 + c])
                    k.act(m_, m_, AF.Sqrt, scale=-1.0, bias=1.0)
                    k.tt(ig_, ig_, xc, ALU.mult)
                    k.tt(ig_, ig_, m_, ALU.mult)
                    k.scan(hl.c(c), a_, ig_, hprev[:, c:c + 1], ALU.mult, ALU.add)
                    k.scan(pc.c(c), a_, k.zeros, pprev[:, c:c + 1], ALU.mult, ALU.add)
                    k.copy(hprev[:, c:c + 1], hl.cs(c, (slice(None), slice(T - 1, T))))
                    k.copy(pprev[:, c:c + 1], pc.cs(c, (slice(None), slice(T - 1, T))))
                    k.copy(rx_ext.cs(c, (slice(None), slice(0, 3))), rx_ext.cs(c, (slice(None), slice(T, T + 3))), eng="pool")
                k.dma(o_hloc[:, :, t0:t0 + T], hl, sem="st_hl")
                k.dma(o_pcum[:, :, t0:t0 + T], pc, sem="st_pc")

            k.ck(7)
            link = k.sb([128, NLINK], F32)
            k.copy(link[:, 0:256], Tl(S_st.ap.rearrange("p c v -> p (c v)"), S_st.bufs))
            k.copy(link[:, 256:258], Dprev)
            k.copy(link[:, 258:260], hprev)
            k.copy(link[:, 260:262], pprev)
            k.dma(o_link, link)

        except StopBuild:
            pass
        k.finish([o_opart, o_qtil, o_hloc, o_pcum, o_link])
    return nc


def build_B(NT):
    NTOK = NT * T
    nc = bass.Bass("TRN2", target_bir_lowering=False)
    with ExitStack() as st:
        k = KB(nc, st)
        xT = k.dram_in("xT", [128, KC, NTOK])
        xh = k.dram_in("xh", [128, KC, 4])
        cvd = k.dram_in("cvec", [128, NV])
        w_in = k.dram_in("w_in", [128, KC, 2832])
        w_gate = k.dram_in("w_gate", [128, KC, 4096])
        w_br = k.dram_in("w_br", [128, 8, 1024])
        w_mix = k.dram_in("w_mix", [128, KC, 1024])
        w_q = k.dram_in("w_q", [128, KC, 1024])
        w_kv = k.dram_in("w_kv", [128, KC, 2048])
        w_o = k.dram_in("w_o", [128, KC, 1024])
        memT = k.dram_in("memT", [128, KC, 256])
        dscd = k.dram_in("diag_sc", [128, 6, 128])
        wmd = k.dram_in("WmT", [128, 4, 128])
        lnGd = k.dram_in("lnG", [128, 256])
        lnBd = k.dram_in("lnB", [128, 256])
        bsd = k.dram_in("bsT", [128, 2, 128])
        i_opart = k.dram_in("opart", [128, 2, NTOK])
        i_qtil = k.dram_in("qtil", [128, 2, NTOK])
        i_hloc = k.dram_in("hloc", [128, 2, NTOK])
        i_pcum = k.dram_in("pcum", [128, 2, NTOK])
        lkd = k.dram_in("links", [128, 8, NLINK])
        seld = k.dram_in("sel", [128, 24])
        xo = k.dram_out("xo", [128, KC, NTOK])

        k.init_psum()
        k.consts()
        k.init_ring(4)
        cv = k.sb([128, NV], F32); k.dma(cv, cvd)
        dsc = k.sb([128, 6, 128], BF16); k.dma(dsc, dscd, q="pool")
        wmf = k.sb([128, 4, 128], F32); k.dma(wmf, wmd)
        lnG = k.sb([128, 256], F32); k.dma(lnG, lnGd)
        lnB = k.sb([128, 256], F32); k.dma(lnB, lnBd)
        bsT = k.sb([128, 2, 128], F32); k.dma(bsT, bsd)
        lk = k.sb([128, 8, NLINK], F32); k.dma(lk, lkd)
        sl = k.sb([128, 24], F32); k.dma(sl, seld)
        triu = k.sb([128, 128], F32)
        k.memset(triu, 1.0)
        k.aselect(triu, [[1, 128]], ALU.is_ge, 0, -1)
        wm = k.sb([128, 4, 128], BF16)
        for g in range(4):
            k.tt(wm[:, g, :], wmf[:, g, :], triu, ALU.mult)
        onesbd = k.sb([128, 128], BF16)
        k.memset(onesbd, 1.0)
        k.aselect(onesbd[:, 0:64], [[0, 64]], ALU.is_ge, 63, -1)
        k.aselect(onesbd[:, 64:128], [[0, 64]], ALU.is_ge, -64, 1)

        Lj = k.sb([128, NLINK], F32)
        St = k.sb([128, 2, 128], F32, nchunk=2)
        hi = k.sb([128, 2], F32)
        k.memset(St, 0.0)
        k.memset(hi, 0.0)
        for j in range(3):
            k.ts(Lj, lk[:, 0, :], sl[:, j * 8:j * 8 + 1], None, ALU.mult)
            for r in range(1, 8):
                k.stt(Lj, lk[:, r, :], sl[:, j * 8 + r:j * 8 + r + 1], Lj, ALU.mult, ALU.add)
            for c in range(2):
                k.stt(St.c(c), St.c(c), Lj[:, 256 + c:257 + c], Lj[:, c * 128:(c + 1) * 128], ALU.mult, ALU.add)
            k.tt(hi, hi, Lj[:, 260:262], ALU.mult)
            k.tt(hi, hi, Lj[:, 258:260], ALU.add)
        Sib = k.sb([128, 2, 128], BF16)
        k.copy(Sib, Tl(St.ap, St.bufs))

        x = k.sb([128, KC, T], F32, nchunk=KC)
        h = k.sb([128, KC, T], BF16, nchunk=KC)
        sq = k.sb([128, KC, T], BF16, nchunk=KC)
        rstd = k.sb([128, T], F32)
        tmp = k.sb([128, T], F32)
        ybuf = k.sb([128, KC, T], F32, nchunk=KC)
        xhs = k.sb([128, KC, 4], F32, nchunk=KC)
        hh = k.sb([128, KC, 4], BF16, nchunk=KC)
        u_ext = k.sb([128, 2, 2 + T], BF16, nchunk=2)
        xs = [x.c(kc) for kc in range(KC)]

        mem = Tl(ybuf.ap[:, :, 0:256], ybuf.cb, ybuf.cb)
        memn = k.sb([128, KC, 256], BF16, nchunk=KC)
        k.dma(mem, memT)
        k.rmsnorm_h(memn, [mem.c(kc) for kc in range(KC)], 256, C_G + 4 * 8, cv, sq, rstd)
        KT = k.sb([128, KC, 256], BF16, nchunk=KC)
        Vt = k.sb([128, 2, 1024], BF16, nchunk=2)
        for g in range(2):
            wk = k.wload(w_kv, KC, 512, 0, g * 512)
            for j in range(4):
                p = k.ps()
                k.proj(p, wk, j * 128, 128, memn, 256)
                k.act(KT.c(g * 4 + j), p[:, 0:256], AF.Copy)
        for g in range(2):
            wv = k.wload(w_kv, KC, 512, 0, 1024 + g * 512)
            for mc in range(2):
                p = k.ps()
                for kc in range(KC):
                    k.mm(p, memn.cs(kc, (slice(None), slice(mc * 128, (mc + 1) * 128))), wv[:, kc, :], start=(kc == 0), stop=(kc == KC - 1))
                k.act(Vt.cs(mc, (slice(None), slice(g * 512, (g + 1) * 512))), p, AF.Copy)

        k.dma(xhs, xh)
        k.rmsnorm_h(hh, [xhs.c(kc) for kc in range(KC)], 4, C_G + 0, cv, sq, rstd)
        wg2 = k.wload(w_in, KC, 512, 0, WI_AB)
        wg3 = k.wload(w_in, KC, 512, 0, WI_AX)
        acs = k.sb([128, T], F32)
        abs_ = k.sb([128, T], F32)
        for c in range(2):
            p = k.ps()
            k.proj(p, wg2, 256 + c * 128, 128, hh, 4)
            k.act(acs[:, 0:4], p[:, 0:4], AF.Copy)
            p2 = k.ps()
            k.proj(p2, wg3, c * 128, 128, hh, 4)
            k.tt(u_ext.cs(c, (slice(None), slice(0, 2))), p2[:, 2:4], acs[:, 2:4], ALU.mult)

        ya = k.sb([128, 2, T], BF16, nchunk=2)
        yb = k.sb([128, 2, T], BF16, nchunk=2)
        yc = k.sb([128, 2, T], BF16, nchunk=2)
        yd = k.sb([128, 2, T], BF16, nchunk=2)
        opart = k.sb([128, 2, T], F32, nchunk=2)
        qtil = k.sb([128, 2, T], F32, nchunk=2)
        qtb = k.sb([128, 2, T], BF16, nchunk=2)
        hloc = k.sb([128, 2, T], F32, nchunk=2)
        pcum = k.sb([128, 2, T], F32, nchunk=2)
        sr = k.sb([128, T], F32)
        o32 = k.sb([128, T], F32)
        sqb = k.sb([128, T], BF16)
        rs = k.sb([128, T], F32)
        t1 = k.sb([128, T], F32)
        sus = k.sb([128, 2, T], F32, nchunk=2)
        sv32 = k.sb([128, 256], F32)
        junk = k.sb([128, 256], F32)
        st4 = k.sb([128, 4], F32)
        vn = k.sb([128, 256], F32)
        vnb = [k.sb([128, 256], BF16) for _ in range(2)]
        m1 = t1
        gg = sr
        hf = o32
        gts = [k.sb([128, T], F32) for _ in range(2)]
        prods = [k.sb([128, T], BF16) for _ in range(2)]
        merged = k.sb([128, KC, T], BF16, nchunk=KC)
        qT = merged
        oT = k.sb([128, KC, T], BF16, nchunk=KC)
        E = k.sb([128, 2, T], BF16, nchunk=2)
        rden = k.sb([128, T], F32)

        for it in range(NT):
            t0 = it * T
            k.dma(x, xT[:, :, t0:t0 + T], sem="xld")
            k.dma(opart, i_opart[:, :, t0:t0 + T], sem="ld_op")
            k.dma(qtil, i_qtil[:, :, t0:t0 + T], sem="ld_qt")
            k.dma(hloc, i_hloc[:, :, t0:t0 + T], sem="ld_hl")
            k.dma(pcum, i_pcum[:, :, t0:t0 + T], sem="ld_pc")
            k.rmsnorm_h(h, xs, T, C_G + 0, cv, sq, rstd)
            if it > 0:
                wg2 = k.wload(w_in, KC, 512, 0, WI_AB)
                wg3 = k.wload(w_in, KC, 512, 0, WI_AX)
            for c in range(2):
                p = k.ps()
                k.proj(p, wg2, 256 + c * 128, 128, h, T)
                k.act(acs, p, AF.Copy)
                p2 = k.ps()
                k.proj(p2, wg3, c * 128, 128, h, T)
                k.tt(u_ext.cs(c, (slice(None), slice(2, 2 + T))), p2, acs, ALU.mult)
                pcv = k.ps()
                for tap in range(3):
                    k.mm(pcv, dsc[:, tap * 2 + c, :], u_ext.cs(c, (slice(None), slice(tap, tap + T))), start=(tap == 0), stop=(tap == 2))
                p3 = k.ps()
                k.proj(p3, wg2, c * 128, 128, h, T)
                k.act(abs_, p3, AF.Copy)
                k.tt(ya.c(c), pcv, abs_, ALU.mult)
                k.copy(u_ext.cs(c, (slice(None), slice(0, 2))), u_ext.cs(c, (slice(None), slice(T, T + 2))), eng="pool")
            for c in range(2):
                k.copy(qtb.c(c), qtil.c(c))
                p = k.ps()
                k.proj(p, wg3, 256 + c * 128, 128, h, T)
                k.act(sr, p, AF.Silu)
                for hh_ in range(2):
                    hr = hh_ * 64
                    pc_ = k.ps()
                    k.mm(pc_[hr:hr + 64, :], Sib[hr:hr + 64, c, hr:hr + 64], qtb.cs(c, (slice(hr, hr + 64), slice(None))),
                         start=True, stop=True, tp=(hr, hr))
                    k.tt(o32[hr:hr + 64, :], pc_[hr:hr + 64, :], opart.cs(c, (slice(hr, hr + 64), slice(None))), ALU.add)
                k.act(sqb, o32, AF.Square)
                pss = k.ps()
                k.mm(pss, onesbd, sqb)
                k.act(rs, pss, AF.Sqrt, bias=EPS, scale=1.0 / 64.0)
                k.recip(rs, rs)
                k.stt(t1, o32, cv[:, C_GNG + c:C_GNG + c + 1], rs, ALU.mult, ALU.mult)
                k.tt(yb.c(c), t1, sr, ALU.mult)
            wg4 = k.wload(w_in, KC, 512, 0, WI_SU)
            for c in range(2):
                p = k.ps()
                k.proj(p, wg4, c * 128, 128, h, T)
                k.act(sus.c(c), p, AF.Copy)
            pmix = [k.ps(hold=True), k.ps(hold=True)]
            for b in range(4):
                p = k.ps()
                for kc in range(KC):
                    k.mm(p[:, 0:256], h.cs(kc, (slice(None), slice(b * 128, (b + 1) * 128))), wg4[:, kc, 256:512],
                         start=(kc == 0), stop=(kc == KC - 1))
                k.memset(st4, 0.0, eng="dve")
                k.act(sv32, p[:, 0:256], AF.Identity, accum=st4[:, 0:1])
                k.ts(st4[:, 1:2], st4[:, 0:1], -1.0 / 256.0, None, ALU.mult)
                k.act(junk, sv32, AF.Square, bias=st4[:, 1:2], accum=st4[:, 2:3])
                k.act(st4[:, 3:4], st4[:, 2:3], AF.Sqrt, bias=EPS, scale=1.0 / 256.0)
                k.recip(st4[:, 3:4], st4[:, 3:4])
                k.ts(vn, sv32, st4[:, 1:2], st4[:, 3:4], ALU.add, ALU.mult)
                k.tt(vn, vn, lnG, ALU.mult)
                vb = vnb[b % 2]
                k.tt(vb, vn, lnB, ALU.add)
                for g in range(4):
                    c, hr = g // 2, (g % 2) * 64
                    k.mm(pmix[c][hr:hr + 64, b * 128:(b + 1) * 128], vb[:, g * 64:(g + 1) * 64], wm[:, g, :],
                         start=True, stop=True, tp=(0, hr))
            for c in range(2):
                k.tt(Tl(m1.ap.rearrange("p (b i) -> p b i", i=128), m1.bufs),
                     Tl(pmix[c].ap.rearrange("p (b i) -> p b i", i=128), pmix[c].bufs),
                     Tl(bsT.ap[:, c, :].unsqueeze(1).to_broadcast([128, 4, 128]), bsT.bufs), ALU.add)
                k.tt(yc.c(c), sus.c(c), m1, ALU.mult)
                k.release(pmix[c])
            wg5 = k.wload(w_in, KC, 256, 0, WI_RG)
            for c in range(2):
                p = k.ps()
                k.proj(p, wg5, c * 128, 128, h, T)
                k.act(gg, p, AF.Gelu)
                k.stt(hf, pcum.c(c), hi[:, c:c + 1], hloc.c(c), ALU.mult, ALU.add)
                k.tt(yd.c(c), hf, gg, ALU.mult)
            ys = [ya, yb, yc, yd]
            wbr = None
            for m in range(KC):
                wbr = k.wload(w_br, 8, 128, 0, m * 128)
                wgt = k.wload(w_gate, KC, 512, 0, m * 512)
                col = 0
                mg = k.ps(hold=True)
                for kb in range(4):
                    pb = k.ps()
                    for kc in range(2):
                        k.mm(pb, wbr[:, kb * 2 + kc, col:col + 128], ys[kb].c(kc), start=(kc == 0), stop=(kc == 1))
                    pg = k.ps()
                    k.proj(pg, wgt, kb * 128, 128, h, T)
                    gt = gts[kb % 2]
                    pr = prods[kb % 2]
                    k.act(gt, pg, AF.Sigmoid, bias=cv[:, C_BG + kb * 8 + m:C_BG + kb * 8 + m + 1])
                    k.tt(pr, pb, gt, ALU.mult)
                    k.mm(mg, k.ident, pr, start=(kb == 0), stop=(kb == 3))
                k.act(merged.c(m), mg, AF.Copy)
                k.release(mg)
            wmx = None
            for m in range(KC):
                if m % 4 == 0:
                    wmx = k.wload(w_mix, KC, 512, 0, (m // 4) * 512)
                p = k.ps()
                k.proj(p, wmx, (m % 4) * 128, 128, merged, T)
                k.act(ybuf.c(m), p, AF.Copy)
            k.postnorm_residual(xs, ybuf, C_G + 1 * 8, cv, sq, rstd, tmp)
            for _xa in ([] if SKIP_XA[0] else [0]):
                k.rmsnorm_h(h, xs, T, C_G + 2 * 8, cv, sq, rstd)
                wqs = None
                for m in range(KC):
                    if m % 4 == 0:
                        wqs = k.wload(w_q, KC, 512, 0, (m // 4) * 512)
                    p = k.ps()
                    k.proj(p, wqs, (m % 4) * 128, 128, h, T)
                    k.act(qT.c(m), p, AF.Copy)
                for hd in range(4):
                    for mc in range(2):
                        p = k.ps()
                        for dc in range(2):
                            k.mm(p, KT.cs(2 * hd + dc, (slice(None), slice(mc * 128, (mc + 1) * 128))), qT.c(2 * hd + dc),
                                 start=(dc == 0), stop=(dc == 1))
                        k.act(E.c(mc), p, AF.Exp, scale=1.0 / 16.0)
                    pden = k.ps()
                    for mc in range(2):
                        k.mm(pden, k.ones, E.c(mc), start=(mc == 0), stop=(mc == 1))
                    k.recip(rden, pden)
                    for dc in range(2):
                        po = k.ps()
                        for mc in range(2):
                            k.mm(po, Vt.cs(mc, (slice(None), slice((2 * hd + dc) * 128, (2 * hd + dc + 1) * 128))), E.c(mc),
                                 start=(mc == 0), stop=(mc == 1))
                        k.tt(oT.c(2 * hd + dc), po, rden, ALU.mult)
                wos = None
                for m in range(KC):
                    if m % 4 == 0:
                        wos = k.wload(w_o, KC, 512, 0, (m // 4) * 512)
                    p = k.ps()
                    k.proj(p, wos, (m % 4) * 128, 128, oT, T)
                    k.act(ybuf.c(m), p, AF.Copy)
                k.postnorm_residual(xs, ybuf, C_G + 3 * 8, cv, sq, rstd, tmp)
            k.dma(xo[:, :, t0:t0 + T], x, sem="st_x")
        k.finish([xo])
    return nc


def build_C(NT):
    NTOK = NT * T
    nc = bass.Bass("TRN2", target_bir_lowering=False)
    with ExitStack() as st:
        k = KB(nc, st)
        xT = k.dram_in("xT", [128, KC, NTOK])
        xh = k.dram_in("xh", [128, KC, 4])
        cvd = k.dram_in("cvec", [128, NV])
        w_up = k.dram_in("w_up", [128, KC, 2 * DFF])
        dffd = k.dram_in("diag_ffn", [128, NFC * 3, 128])
        w_dn = k.dram_in("w_dn", [128, 22, 1024])
        xo = k.dram_out("xo", [128, KC, NTOK])
        k.init_psum()
        k.consts()
        k.init_ring(5)
        cv = k.sb([128, NV], F32); k.dma(cv, cvd)
        x = k.sb([128, KC, T], F32, nchunk=KC)
        h = k.sb([128, KC, T], BF16, nchunk=KC)
        sq = k.sb([128, KC, T], BF16, nchunk=KC)
        rstd = k.sb([128, T], F32)
        tmp = k.sb([128, T], F32)
        ybuf = k.sb([128, KC, T], F32, nchunk=KC)
        xhs = k.sb([128, KC, 4], F32, nchunk=KC)
        hh = k.sb([128, KC, 4], BF16, nchunk=KC)
        hal = [k.sb([128, NFC, 2], BF16) for _ in range(2)]
        us = [k.sb([128, T], BF16) for _ in range(3)]
        gls = [k.sb([128, T], F32) for _ in range(2)]
        hmid = k.sb([128, 22, T], BF16, nchunk=22)
        xs = [x.c(kc) for kc in range(KC)]

        k.dma(xhs, xh)
        k.rmsnorm_h(hh, [xhs.c(kc) for kc in range(KC)], 4, C_G + 5 * 8, cv, sq, rstd)
        for g in range(11):
            wup = k.wload(w_up, KC, 512, 0, g * 512)
            for j in range(4):
                ch = g * 4 + j
                p = k.ps()
                k.proj(p, wup, j * 128, 128, hh, 4)
                k.act(hal[0][:, ch, :], p[:, 2:4], AF.Copy)

        ui = 0
        for it in range(NT):
            t0 = it * T
            cur, nxt = hal[it % 2], hal[(it + 1) % 2]
            k.dma(x, xT[:, :, t0:t0 + T], sem="xld")
            k.rmsnorm_h(h, xs, T, C_G + 5 * 8, cv, sq, rstd)
            for g in range(11):
                wup = k.wload(w_up, KC, 512, 0, g * 512)
                dff = k.wload(dffd, 12, 128, g * 12, 0)
                for j in range(4):
                    ch = g * 4 + j
                    p = k.ps()
                    k.proj(p, wup, j * 128, 128, h, T)
                    u = us[ui % 3]
                    ui += 1
                    k.act(u, p, AF.Copy)
                    k.act(nxt[:, ch, :], p[:, T - 2:T], AF.Copy)
                    pc_ = k.ps()
                    d0, d1, d2 = dff[:, j * 3 + 0, :], dff[:, j * 3 + 1, :], dff[:, j * 3 + 2, :]
                    k.mm(pc_[:, 0:T], d2, u[:, 0:T], start=True, stop=False)
                    k.mm(pc_[:, 1:T], d1, u[:, 0:T - 1], start=False, stop=False)
                    k.mm(pc_[:, 0:1], d1, cur[:, ch, 1:2], start=False, stop=False)
                    k.mm(pc_[:, 2:T], d0, u[:, 0:T - 2], start=False, stop=False)
                    k.mm(pc_[:, 0:2], d0, cur[:, ch, 0:2], start=False, stop=True)
                    fcb = cv[:, C_FCB + ch:C_FCB + ch + 1]
                    if j % 2 == 0:
                        k.act(gls[(ch // 2) % 2], pc_, AF.Gelu, bias=fcb)
                    else:
                        k.stt(hmid.c(ch // 2), pc_, fcb, gls[(ch // 2) % 2], ALU.add, ALU.mult)
            for cg in range(2):
                accs = [k.ps(hold=True) for _ in range(4)]
                for (r0, rows) in ((0, 8), (8, 8), (16, 6)):
                    wd = k.wload(w_dn, rows, 512, r0, cg * 512)
                    for mi in range(4):
                        for r in range(rows):
                            k.mm(accs[mi], wd[:, r, mi * 128:(mi + 1) * 128], hmid.c(r0 + r), start=(r0 + r == 0), stop=(r0 + r == 21))
                for mi in range(4):
                    k.act(ybuf.c(cg * 4 + mi), accs[mi], AF.Copy)
                    k.release(accs[mi])
            k.postnorm_residual(xs, ybuf, C_G + 6 * 8, cv, sq, rstd, tmp)
            k.dma(xo[:, :, t0:t0 + T], x, sem="st_x")
        k.finish([xo])
    return nc


def _fm(w):
    K, N = w.shape
    return np.ascontiguousarray(w.reshape(K // 128, 128, N).transpose(1, 0, 2))


def _cv(v):
    return np.asarray(v).reshape(-1, 128).T


def _tok_fm(a):
    n = a.shape[0]
    return np.ascontiguousarray(a.T.reshape(KC, 128, n).transpose(1, 0, 2))


def _diag(v):
    m = np.zeros((128, 128), np.float32)
    m[np.arange(128), np.arange(128)] = v
    return m


_PERM = np.concatenate([np.concatenate([np.arange(q * 128, (q + 1) * 128), DFF + np.arange(q * 128, (q + 1) * 128)])
                        for q in range(22)])


def prep_layer(inp, l):
    f = lambda k_: np.asarray(inp[k_], dtype=np.float32)[l]
    cv = np.zeros((128, NV), np.float32)
    ng = f("norm_g")
    for i in range(7):
        cv[:, C_G + i * 8:C_G + (i + 1) * 8] = _cv(ng[i])
    cv[:, C_GNG:C_GNG + 2] = _cv(f("gla_norm_g"))
    cv[:, C_LCB:C_LCB + 2] = _cv(f("lru_conv_b"))
    cv[:, C_LBA:C_LBA + 2] = _cv(f("lru_b_a"))
    cv[:, C_LBX:C_LBX + 2] = _cv(f("lru_b_x"))
    cv[:, C_LAM:C_LAM + 2] = _cv(f("lru_lambda"))
    bg = f("b_gate")
    for kb in range(4):
        cv[:, C_BG + kb * 8:C_BG + (kb + 1) * 8] = _cv(bg[kb])
    cv[:, C_FCB:C_FCB + NFC] = _cv(f("ffn_conv_b")[_PERM])
    wi = f("w_in")
    sp = {"a_b": (0, 256), "a_c": (256, 512), "a_x": (512, 768), "q": (768, 1024), "k": (1024, 1280), "v": (1280, 1536),
          "r": (1536, 1792), "a_lr": (1792, 1808), "su": (1808, 2064), "sv": (2064, 2320), "rx": (2320, 2576), "rg": (2576, 2832)}
    order = ["q", "rx", "k", "v", "a_b", "a_c", "a_x", "r", "su", "sv", "rg", "a_lr"]
    w_in_r = np.concatenate([wi[:, sp[n][0]:sp[n][1]] for n in order], axis=1)
    out = {"cvec": cv, "w_in": _fm(w_in_r)}
    out["w_gate"] = _fm(np.ascontiguousarray(f("w_gate").reshape(D, 4, 8, 128).transpose(0, 2, 1, 3)).reshape(D, 4096))
    out["w_br"] = np.ascontiguousarray(f("w_branch").reshape(4, 2, 128, D).transpose(2, 0, 1, 3)).reshape(128, 8, D)
    out["w_mix"] = _fm(f("w_mix_out"))
    out["w_q"] = _fm(f("xa_wq"))
    out["w_kv"] = _fm(f("xa_wkv"))
    out["w_o"] = _fm(f("xa_wo"))
    out["w_up"] = _fm(np.ascontiguousarray(f("ffn_w_up")[:, _PERM]))
    out["w_dn"] = _fm(f("ffn_w_down"))
    lcw = f("lru_conv_w")
    out["diag_lru"] = np.stack([_diag(lcw[tap, c * 128:(c + 1) * 128]) for tap in range(4) for c in range(2)], axis=1)
    scw = f("sc_conv_w")
    out["diag_sc"] = np.stack([_diag(scw[tap, c * 128:(c + 1) * 128]) for tap in range(3) for c in range(2)], axis=1)
    fcw = f("ffn_conv_w")[:, _PERM]
    out["diag_ffn"] = np.stack([_diag(fcw[tap, ch * 128:(ch + 1) * 128]) for ch in range(NFC) for tap in range(3)], axis=1)
    for nm, key in (("bd_a", "lru_w_a"), ("bd_x", "lru_w_x")):
        w = f(key)
        bd = np.zeros((128, 2, 128), np.float32)
        for c in range(2):
            bd[0:64, c, 0:64] = w[2 * c]
            bd[64:128, c, 64:128] = w[2 * c + 1]
        out[nm] = bd
    out["WmT"] = np.ascontiguousarray(f("sgu_w").transpose(2, 0, 1))
    out["lnG"] = np.ascontiguousarray(np.broadcast_to(f("sgu_ln_g")[None, :], (128, 256)))
    out["lnB"] = np.ascontiguousarray(np.broadcast_to(f("sgu_ln_b")[None, :], (128, 256)))
    sb_ = f("sgu_b")
    bsT = np.zeros((128, 2, 128), np.float32)
    for c in range(2):
        bsT[0:64, c, :] = sb_[2 * c][None, :]
        bsT[64:128, c, :] = sb_[2 * c + 1][None, :]
    out["bsT"] = bsT
    out["walpha"] = np.ascontiguousarray(f("gla_w_alpha"))
    out["balpha"] = np.ascontiguousarray(f("gla_b_alpha")[None, :])
    return {k_: np.ascontiguousarray(v, dtype=np.float32) for k_, v in out.items()}


_PROGS = {}


def _prog(name, NT):
    key = (name, NT)
    if key not in _PROGS:
        _PROGS[key] = {"A": build_A, "B": build_B, "C": build_C}[name](NT)
    return _PROGS[key]


def _halo(xcores, NC_PER):
    out = []
    for c in range(len(xcores)):
        if c % NC_PER == 0:
            out.append(np.zeros((128, KC, 4), np.float32))
        else:
            out.append(np.ascontiguousarray(xcores[c - 1][:, :, -4:]))
    return out


A_KEYS = ("cvec", "w_in", "diag_lru", "bd_a", "bd_x", "walpha", "balpha")
B_KEYS = ("cvec", "w_in", "w_gate", "w_br", "w_mix", "w_q", "w_kv", "w_o", "diag_sc", "WmT", "lnG", "lnB", "bsT")
C_KEYS = ("cvec", "w_up", "diag_ffn", "w_dn")


def kernel_unfused(**inputs):
    x = np.asarray(inputs["x"], dtype=np.float32)
    mem = np.asarray(inputs["mem"], dtype=np.float32)
    B_, S_, _ = x.shape
    NCORE = 8
    NC_PER = NCORE // B_
    NTOK = S_ // NC_PER
    NT = NTOK // T
    cores = list(range(NCORE))
    xc = [_tok_fm(x[c // NC_PER, (c % NC_PER) * NTOK:(c % NC_PER + 1) * NTOK]) for c in cores]
    memT = [_tok_fm(mem[b]) for b in range(B_)]
    sels = []
    for c in cores:
        r, gb = c % NC_PER, (c // NC_PER) * NC_PER
        s = np.zeros((3, 8), np.float32)
        for j in range(3):
            src = r - 3 + j
            if src >= 0:
                s[j, gb + src] = 1.0
        sels.append(np.ascontiguousarray(np.broadcast_to(s.reshape(1, 24), (128, 24))))
    depth = np.asarray(inputs["norm_g"]).shape[0]
    for l in range(depth):
        P = prep_layer(inputs, l)
        xh = _halo(xc, NC_PER)
        resA = run_bass_kernel_spmd(_prog("A", NT), [dict({k_: P[k_] for k_ in A_KEYS}, xT=xc[c], xh=xh[c]) for c in cores],
                                    core_ids=cores).results
        links = np.ascontiguousarray(np.stack([np.asarray(resA[c]["link"]) for c in cores], axis=1))
        resB = run_bass_kernel_spmd(
            _prog("B", NT),
            [dict({k_: P[k_] for k_ in B_KEYS}, xT=xc[c], xh=xh[c], memT=memT[c // NC_PER],
                  opart=np.asarray(resA[c]["opart"]), qtil=np.asarray(resA[c]["qtil"]),
                  hloc=np.asarray(resA[c]["hloc"]), pcum=np.asarray(resA[c]["pcum"]), links=links, sel=sels[c]) for c in cores],
            core_ids=cores).results
        xc = [np.asarray(resB[c]["xo"]) for c in cores]
        xh = _halo(xc, NC_PER)
        resC = run_bass_kernel_spmd(_prog("C", NT), [dict({k_: P[k_] for k_ in C_KEYS}, xT=xc[c], xh=xh[c]) for c in cores],
                                    core_ids=cores).results
        xc = [np.asarray(resC[c]["xo"]) for c in cores]
    out = np.zeros((B_, S_, D), np.float32)
    for c in cores:
        out[c // NC_PER, (c % NC_PER) * NTOK:(c % NC_PER + 1) * NTOK] = xc[c].transpose(2, 1, 0).reshape(NTOK, D)
    return out


class NS:
    pass


def _sl(a, b):
    return (slice(None), slice(a, b))


def fused_phase_A(k, G, l, NT):
    cv = G.cv
    st = ExitStack()
    k.cur = st
    k.lyr = l
    k.S.barrier()
    k.init_ring(5)
    dl = k.sb([128, 8, 128], BF16); k.dma(dl, Tl(G.dlru.ap[l], G.dlru.bufs), q="pool", sem="cA1")
    wa = k.sb([128, 2, 128], BF16); k.dma(wa, Tl(G.bda.ap[l], G.bda.bufs), q="pool", sem="cA2")
    wx = k.sb([128, 2, 128], BF16); k.dma(wx, Tl(G.bdx.ap[l], G.bdx.bufs), q="pool", sem="cA3")
    walb = k.sb([16, 256], BF16); k.dma(walb, Tl(G.wal.ap[l], G.wal.bufs), q="pool", sem="cA4")
    balb = k.sb([1, 256], BF16); k.dma(balb, Tl(G.bal.ap[l], G.bal.bufs), q="pool", sem="cA5")
    trib, tribb, mask4f = G.trib, G.tribb, G.mask4f
    cneg = k.sb([128, 4], F32)
    k.act(cneg[:, 0:2], cv[:, C_LAM:C_LAM + 2], AF.Exp, scale=-1.0)
    k.act(cneg[:, 0:2], cneg[:, 0:2], AF.Ln, bias=1.0)
    k.ts(cneg[:, 2:4], cneg[:, 0:2], -16.0, None, ALU.mult)
    k.ts(cneg[:, 0:2], cneg[:, 0:2], -8.0, None, ALU.mult)
    h = k.sb([128, KC, T], BF16, nchunk=KC)
    sq = k.sb([128, KC, T], BF16, nchunk=KC)
    rstd = k.sb([128, T], F32)
    hh = k.sb([128, KC, 4], BF16, nchunk=KC)
    rx_ext = k.sb([128, 2, 3 + T], BF16, nchunk=2)
    S_st = k.sb([128, 2, 128], F32, nchunk=2); k.memset(S_st, 0.0, eng="dve")
    Dprev = k.sb([128, 2], F32); k.memset(Dprev, 1.0, eng="dve")
    hprev = k.sb([128, 2], F32); k.memset(hprev, 0.0, eng="dve")
    pprev = k.sb([128, 2], F32); k.memset(pprev, 1.0, eng="dve")
    alr = k.sb([16, T], BF16)
    nsp = k.sb([128, 4, 256], F32, nchunk=4)
    nhi = k.sb([128, 4, 256], BF16, nchunk=4)
    nlo = k.sb([128, 4, 256], BF16, nchunk=4)
    ek = k.sb([128, 4, 256], F32, nchunk=4)
    kdec = k.sb([128, 4, 256], BF16, nchunk=4)
    vtok = k.sb([128, 4, 256], BF16, nchunk=4)
    kdecT = k.sb([128, 2, T], BF16, nchunk=2)
    eq = k.sb([128, 2, T], F32, nchunk=2)
    qdecT = k.sb([128, 2, T], BF16, nchunk=2)
    scT = k.sb([128, 4, T], BF16, nchunk=4)
    kvs = k.sb([128, 8, 2, 128], F32, nchunk=8)
    Sb = k.sb([128, 8, 2, 128], BF16, nchunk=8)
    opart = k.sb([128, 2, T], F32, nchunk=2)
    qtil = k.sb([128, 2, T], F32, nchunk=2)
    Dinc = k.sb([128, 2, 9], F32, nchunk=2)
    hl = k.sb([128, 2, T], F32, nchunk=2)
    pc = k.sb([128, 2, T], F32, nchunk=2)
    xc = k.sb([128, T], F32)
    xcb = k.sb([128, T], BF16)
    rg_ = k.sb([128, T], F32)
    ig_ = k.sb([128, T], F32)
    a_ = k.sb([128, T], F32)
    m_ = k.sb([128, T], F32)

    k.rmsnorm_h(hh, [G.xhs.c(kc) for kc in range(KC)], 4, C_G + 0, cv, sq, rstd)
    wg0 = k.wload(G.w_in, KC, 512, 0, 0)
    for c in range(2):
        p = k.ps()
        k.proj(p, wg0, 256 + c * 128, 128, hh, 4)
        k.act(rx_ext.cs(c, _sl(0, 3)), p[:, 1:4], AF.Copy)

    for it in range(NT):
        t0 = it * T
        xs = G.xs[it]
        k.rmsnorm_h(h, xs, T, C_G + 0, cv, sq, rstd)
        if it > 0:
            wg0 = k.wload(G.w_in, KC, 512, 0, 0)
        wg1 = k.wload(G.w_in, KC, 512, 0, 512)
        walr = k.wload(G.w_in, KC, 16, 0, WI_ALR)
        p = k.ps()
        k.proj(p, walr, 0, 16, h, T)
        k.act(alr, p[0:16, :], AF.Copy)
        for b2 in range(2):
            p = k.ps()
            for bb in range(2):
                b = b2 * 2 + bb
                k.mm(p[:, bb * 256:(bb + 1) * 256], alr[:, b * 128:(b + 1) * 128], walb, start=True, stop=False)
                k.mm(p[:, bb * 256:(bb + 1) * 256], k.ones[0:1, :], balb, start=False, stop=True)
            for bb in range(2):
                b = b2 * 2 + bb
                k.act(ek.c(b), p[:, bb * 256:(bb + 1) * 256], AF.Exp, scale=-1.0)
                k.act(nsp.c(b), ek.c(b), AF.Ln, bias=1.0)
        for b2 in range(2):
            p = k.ps()
            for bb in range(2):
                b = b2 * 2 + bb
                k.copy(nhi.c(b), nsp.c(b))
                k.tt(nlo.c(b), nsp.c(b), nhi.c(b), ALU.subtract)
                k.mm(p[:, bb * 256:(bb + 1) * 256], tribb, nhi.c(b), start=True, stop=False)
                k.mm(p[:, bb * 256:(bb + 1) * 256], tribb, nlo.c(b), start=False, stop=True)
            for bb in range(2):
                b = b2 * 2 + bb
                k.act(ek.c(b), p[:, bb * 256:(bb + 1) * 256], AF.Exp, scale=1.0 / 16.0)
        for b in range(4):
            p = k.ps()
            for kc in range(KC):
                k.mm(p, h.cs(kc, _sl(b * 128, (b + 1) * 128)), wg1[:, kc, :], start=(kc == 0), stop=(kc == KC - 1))
            k.tt(kdec.c(b), p[:, 0:256], ek.c(b), ALU.mult)
            k.act(vtok.c(b), p[:, 256:512], AF.Copy)
        for c in range(2):
            for b in range(4):
                k.transpose(k.pst[:, b * 128:(b + 1) * 128], kdec.cs(b, _sl(c * 128, (c + 1) * 128)), k.ident)
            k.copy(kdecT.c(c), k.pst[:, 0:T])
        for c in range(2):
            p = k.ps()
            for b in range(4):
                k.mm(p[:, b * 128:(b + 1) * 128], nhi.cs(b, _sl(c * 128, (c + 1) * 128)), tribb, start=True, stop=False)
                k.mm(p[:, b * 128:(b + 1) * 128], nlo.cs(b, _sl(c * 128, (c + 1) * 128)), tribb, start=False, stop=True)
            k.act(eq.c(c), p, AF.Exp, scale=-1.0 / 16.0)
        for c in range(2):
            p = k.ps()
            k.proj(p, wg0, c * 128, 128, h, T)
            k.stt(qdecT.c(c), p, 0.125, eq.c(c), ALU.mult, ALU.mult)
        for c in range(2):
            p = k.ps()
            k.proj(p, wg0, 256 + c * 128, 128, h, T)
            k.act(rx_ext.cs(c, _sl(3, 3 + T)), p, AF.Copy)
        for hd in range(4):
            c, r0 = hd // 2, (hd % 2) * 64
            p = k.ps()
            for b in range(4):
                k.mm(p[:, b * 128:(b + 1) * 128], kdecT.cs(c, (slice(r0, r0 + 64), slice(b * 128, (b + 1) * 128))),
                     qdecT.cs(c, (slice(r0, r0 + 64), slice(b * 128, (b + 1) * 128))), start=True, stop=True)
            k.tt(scT.c(hd), p, mask4f, ALU.mult)
        for c in range(2):
            for par in range(2):
                p = k.ps()
                r = par * 64
                for j in range(4):
                    k.mm(p[:, j * 128:(j + 1) * 128], kdec.cs(j, (slice(r, r + 64), slice(c * 128, (c + 1) * 128))),
                         vtok.cs(j, (slice(r, r + 64), slice(c * 128, (c + 1) * 128))), start=True, stop=True)
                for j in range(4):
                    n = 2 * j + par
                    k.ts(Tl(kvs.ap[:, n, c, :], (kvs.cb[n],)), p[:, j * 128:(j + 1) * 128],
                         eq.cs(c, _sl(n * 64 + 63, n * 64 + 64)), None, ALU.mult)
        for n in range(8):
            for c in range(2):
                k.act(Tl(Sb.ap[:, n, c, :], (Sb.cb[n],)), S_st.c(c), AF.Copy)
                k.stt(S_st.c(c), S_st.c(c), eq.cs(c, _sl(n * 64 + 63, n * 64 + 64)),
                      Tl(kvs.ap[:, n, c, :], (kvs.cb[n],)), ALU.mult, ALU.add)
        for c in range(2):
            p = k.ps()
            for hh_ in range(2):
                hd = c * 2 + hh_
                hr = hh_ * 64
                for b in range(4):
                    k.mm(p[hr:hr + 64, b * 128:(b + 1) * 128], vtok.cs(b, _sl(hd * 64, hd * 64 + 64)),
                         scT.cs(hd, _sl(b * 128, (b + 1) * 128)), start=True, stop=False, tp=(0, hr))
                    for n in (2 * b, 2 * b + 1):
                        k.mm(p[hr:hr + 64, n * 64:(n + 1) * 64], Tl(Sb.ap[hr:hr + 64, n, c, hr:hr + 64], (Sb.cb[n],)),
                             qdecT.cs(c, (slice(hr, hr + 64), slice(n * 64, (n + 1) * 64))), start=False, stop=True, tp=(hr, hr))
            k.act(opart.c(c), p, AF.Copy)
        k.dma(Tl(G.s_opart.ap[:, :, t0:t0 + T], (G.sb_op[it],)), opart, sem="st_op")
        for c in range(2):
            k.copy(Dinc.cs(c, _sl(0, 1)), Dprev[:, c:c + 1])
            k.scan(Dinc.cs(c, _sl(1, 9)), eq.cs(c, (slice(None), slice(63, T, 64))), k.zeros[:, 0:8],
                   Dprev[:, c:c + 1], ALU.mult, ALU.add)
            k.tt(Tl(qtil.ap[:, c, :].rearrange("p (n i) -> p n i", i=64), (qtil.cb[c],)),
                 Tl(qdecT.ap[:, c, :].rearrange("p (n i) -> p n i", i=64), (qdecT.cb[c],)),
                 Tl(Dinc.ap[:, c, 0:8].unsqueeze(2).to_broadcast([128, 8, 64]), (Dinc.cb[c],)), ALU.mult)
            k.copy(Dprev[:, c:c + 1], Dinc.cs(c, _sl(8, 9)))
        k.dma(Tl(G.s_qtil.ap[:, :, t0:t0 + T], (G.sb_qt[it],)), qtil, sem="st_qt")
        for c in range(2):
            p = k.ps()
            for tap in range(4):
                k.mm(p, dl[:, tap * 2 + c, :], rx_ext.cs(c, _sl(tap, tap + T)), start=(tap == 0), stop=(tap == 3))
            k.act(xc, p, AF.Identity, bias=cv[:, C_LCB + c:C_LCB + c + 1])
            k.act(xcb, p, AF.Identity, bias=cv[:, C_LCB + c:C_LCB + c + 1])
            pr = k.ps()
            k.mm(pr, wa[:, c, :], xcb)
            pi = k.ps()
            k.mm(pi, wx[:, c, :], xcb)
            k.act(rg_, pr, AF.Sigmoid, bias=cv[:, C_LBA + c:C_LBA + c + 1])
            k.act(ig_, pi, AF.Sigmoid, bias=cv[:, C_LBX + c:C_LBX + c + 1])
            k.act(a_, rg_, AF.Exp, scale=cneg[:, c:c + 1])
            k.act(m_, rg_, AF.Exp, scale=cneg[:, 2 + c:3 + c])
            k.act(m_, m_, AF.Sqrt, scale=-1.0, bias=1.0)
            k.tt(ig_, ig_, xc, ALU.mult)
            k.tt(ig_, ig_, m_, ALU.mult)
            k.scan(hl.c(c), a_, ig_, hprev[:, c:c + 1], ALU.mult, ALU.add)
            k.scan(pc.c(c), a_, k.zeros, pprev[:, c:c + 1], ALU.mult, ALU.add)
            k.copy(hprev[:, c:c + 1], hl.cs(c, _sl(T - 1, T)))
            k.copy(pprev[:, c:c + 1], pc.cs(c, _sl(T - 1, T)))
            k.copy(rx_ext.cs(c, _sl(0, 3)), rx_ext.cs(c, _sl(T, T + 3)), eng="dve")
        k.dma(Tl(G.s_hloc.ap[:, :, t0:t0 + T], (G.sb_hl[it],)), hl, sem="st_hl")
        k.dma(Tl(G.s_pcum.ap[:, :, t0:t0 + T], (G.sb_pc[it],)), pc, sem="st_pc")
    link = k.sb([128, NLINK], F32)
    k.copy(link[:, 0:256], Tl(S_st.ap.rearrange("p c v -> p (c v)"), S_st.bufs))
    k.copy(link[:, 256:258], Dprev)
    k.copy(link[:, 258:260], hprev)
    k.copy(link[:, 260:262], pprev)
    k.dma(G.link_src, link, sem="st_lk")
    k.S.flush()
    st.close()
    k.cur = None


def fused_allgather(k, src, dst, sem):
    k.S.custom("pool", lambda e: e.collective_compute("AllGather", ALU.bypass, replica_groups=[list(range(8))],
                                                      ins=[src.ap.opt()], outs=[dst.ap.opt()]),
               reads=list(src.bufs), writes=list(dst.bufs), sem=sem, inc=1)


def fused_halo_exchange(k, G, NT):
    st = ExitStack()
    k.cur = st
    k.S.barrier()
    NTOK = NT * T
    last = G.xs[NT - 1]
    hsb = k.sb([128, KC, 4], F32)
    for kc in range(KC):
        k.copy(hsb[:, kc, :], last[kc][:, T - 4:T])
    k.dma(G.halo_src, Tl(hsb.ap.rearrange("p a b -> p (a b)"), hsb.bufs), sem="st_ha")
    fused_allgather(k, G.halo_src, G.halo_all, "cc_h")
    hg = k.sb([128, 8, 32], F32)
    k.dma(hg, Tl(G.halo_all.ap.rearrange("(r p) n -> p r n", p=128), G.halo_all.bufs), sem="ld_ha")
    xf = Tl(G.xhs.ap.rearrange("p a b -> p (a b)"), G.xhs.cb)
    k.ts(xf, hg[:, 0, :], G.selp[:, 0:1], None, ALU.mult)
    for r in range(1, 8):
        k.stt(xf, hg[:, r, :], G.selp[:, r:r + 1], xf, ALU.mult, ALU.add)
    k.S.flush()
    st.close()
    k.cur = None


def fused_phase_B(k, G, l, NT):
    cv = G.cv
    st0 = ExitStack()
    k.cur = st0
    k.lyr = l
    k.S.barrier()
    k.init_ring(4)
    fused_allgather(k, G.link_src, G.link_all, "cc_l")
    dsc = k.sb([128, 6, 128], BF16); k.dma(dsc, Tl(G.dsc.ap[l], G.dsc.bufs), q="pool", sem="cB")
    wmf = k.sb([128, 4, 128], F32); k.dma(wmf, Tl(G.wmd.ap[l], G.wmd.bufs), sem="cB1")
    lnG = k.sb([128, 256], F32); k.dma(lnG, Tl(G.lnG.ap[l], G.lnG.bufs), sem="cB2")
    lnB = k.sb([128, 256], F32); k.dma(lnB, Tl(G.lnB.ap[l], G.lnB.bufs), sem="cB3")
    bsT = k.sb([128, 2, 128], F32); k.dma(bsT, Tl(G.bsT.ap[l], G.bsT.bufs), sem="cB4")
    wm = k.sb([128, 4, 128], BF16)
    for g in range(4):
        k.tt(wm[:, g, :], wmf[:, g, :], G.triu, ALU.mult)
    Sib = k.sb([128, 2, 128], BF16)
    hi = k.sb([128, 2], F32)
    KT = k.sb([128, KC, 256], BF16, nchunk=KC)
    Vt = k.sb([128, 2, 1024], BF16, nchunk=2)
    u_ext = k.sb([128, 2, 2 + T], BF16, nchunk=2)
    R = k.sb([128, 3 * KC * T], BF16)
    rb = [Buf() for _ in range(3 * KC)]
    def rview(i):
        return Tl(R.ap[:, i * KC * T:(i + 1) * KC * T].rearrange("p (a b) -> p a b", b=T), rb[i * KC:(i + 1) * KC], rb[i * KC:(i + 1) * KC])
    merged, h, oT = rview(0), rview(1), rview(2)
    def yview(i0):
        ap = R.ap[:, i0 * KC * T:(i0 + 2) * KC * T].bitcast(F32).rearrange("p (a b) -> p a b", b=T)
        cb = [None] * KC
        t_ = Tl(ap, rb[i0 * KC:(i0 + 2) * KC])
        t_.cb = [(rb[i0 * KC + 2 * m], rb[i0 * KC + 2 * m + 1]) for m in range(KC)]
        return t_
    ybuf7, ybuf8 = yview(1), yview(0)
    sq = k.sb([128, KC, T], BF16, nchunk=KC)
    rstd = k.sb([128, T], F32)
    hh = k.sb([128, KC, 4], BF16, nchunk=KC)

    st1 = ExitStack()
    k.cur = st1
    lk = k.sb([128, 8, NLINK], F32)
    k.dma(lk, Tl(G.link_all.ap.rearrange("(r p) n -> p r n", p=128), G.link_all.bufs), sem="ld_lk")
    Lj = k.sb([128, NLINK], F32)
    St = k.sb([128, 2, 128], F32, nchunk=2)
    k.memset(St, 0.0, eng="dve")
    k.memset(hi, 0.0, eng="dve")
    for j in range(3):
        k.ts(Lj, lk[:, 0, :], G.sel[:, j * 8:j * 8 + 1], None, ALU.mult)
        for r in range(1, 8):
            k.stt(Lj, lk[:, r, :], G.sel[:, j * 8 + r:j * 8 + r + 1], Lj, ALU.mult, ALU.add)
        for c in range(2):
            k.stt(St.c(c), St.c(c), Lj[:, 256 + c:257 + c], Lj[:, c * 128:(c + 1) * 128], ALU.mult, ALU.add)
        k.tt(hi, hi, Lj[:, 260:262], ALU.mult)
        k.tt(hi, hi, Lj[:, 258:260], ALU.add)
    k.copy(Sib, Tl(St.ap, St.bufs))
    mem = k.sb([128, KC, 256], F32, nchunk=KC)
    memn = k.sb([128, KC, 256], BF16, nchunk=KC)
    k.dma(mem, G.memT, sem="ld_mem")
    k.rmsnorm_h(memn, [mem.c(kc) for kc in range(KC)], 256, C_G + 4 * 8, cv, sq, rstd)
    for g in range(2):
        wk = k.wload(G.w_kv, KC, 512, 0, g * 512)
        for j in range(4):
            p = k.ps()
            k.proj(p, wk, j * 128, 128, memn, 256)
            k.act(KT.c(g * 4 + j), p[:, 0:256], AF.Copy)
    for g in range(2):
        wv = k.wload(G.w_kv, KC, 512, 0, 1024 + g * 512)
        for mc in range(2):
            p = k.ps()
            for kc in range(KC):
                k.mm(p, memn.cs(kc, _sl(mc * 128, (mc + 1) * 128)), wv[:, kc, :], start=(kc == 0), stop=(kc == KC - 1))
            k.act(Vt.cs(mc, _sl(g * 512, (g + 1) * 512)), p, AF.Copy)
    k.S.flush()
    st1.close()
    k.cur = st0
    k.S.barrier()

    f32t = [k.sb([128, T], F32) for _ in range(6)]
    acs, abs_, sr, o32, rs, t1 = f32t
    tmp, rden, gg, hf, m1 = acs, abs_, sr, o32, t1
    gts = [rs, t1]
    ya = k.sb([128, 2, T], BF16, nchunk=2)
    yb = k.sb([128, 2, T], BF16, nchunk=2)
    yc = k.sb([128, 2, T], BF16, nchunk=2)
    yd = k.sb([128, 2, T], BF16, nchunk=2)
    ld1 = k.sb([128, 2, T], F32, nchunk=2)
    ld2 = k.sb([128, 2, T], F32, nchunk=2)
    qtb = k.sb([128, 2, T], BF16, nchunk=2)
    sqb = k.sb([128, T], BF16)
    sus = k.sb([128, 2, T], F32, nchunk=2)
    sv32 = k.sb([128, 256], F32)
    junk = k.sb([128, 256], F32)
    st4 = k.sb([128, 4], F32)
    vn = k.sb([128, 256], F32)
    vnb = [k.sb([128, 256], BF16) for _ in range(2)]
    prods = [k.sb([128, T], BF16) for _ in range(4)]
    Es = [k.sb([128, 2, T], BF16, nchunk=2) for _ in range(2)]
    qT = merged

    k.rmsnorm_h(hh, [G.xhs.c(kc) for kc in range(KC)], 4, C_G + 0, cv, sq, rstd)
    wg2 = k.wload(G.w_in, KC, 512, 0, WI_AB)
    wg3 = k.wload(G.w_in, KC, 512, 0, WI_AX)
    for c in range(2):
        p = k.ps()
        k.proj(p, wg2, 256 + c * 128, 128, hh, 4)
        k.act(acs[:, 0:4], p[:, 0:4], AF.Copy)
        p2 = k.ps()
        k.proj(p2, wg3, c * 128, 128, hh, 4)
        k.tt(u_ext.cs(c, _sl(0, 2)), p2[:, 2:4], acs[:, 2:4], ALU.mult)

    for it in range(NT):
        t0 = it * T
        xs = G.xs[it]
        opart, qtil = ld1, ld2
        k.dma(ld1, Tl(G.s_opart.ap[:, :, t0:t0 + T], (G.sb_op[it],)), sem="ld_1")
        k.dma(ld2, Tl(G.s_qtil.ap[:, :, t0:t0 + T], (G.sb_qt[it],)), sem="ld_2")
        k.rmsnorm_h(h, xs, T, C_G + 0, cv, sq, rstd)
        if it > 0:
            wg2 = k.wload(G.w_in, KC, 512, 0, WI_AB)
            wg3 = k.wload(G.w_in, KC, 512, 0, WI_AX)
        for c in range(2):
            p = k.ps()
            k.proj(p, wg2, 256 + c * 128, 128, h, T)
            k.act(acs, p, AF.Copy)
            p2 = k.ps()
            k.proj(p2, wg3, c * 128, 128, h, T)
            k.tt(u_ext.cs(c, _sl(2, 2 + T)), p2, acs, ALU.mult)
            pcv = k.ps()
            for tap in range(3):
                k.mm(pcv, dsc[:, tap * 2 + c, :], u_ext.cs(c, _sl(tap, tap + T)), start=(tap == 0), stop=(tap == 2))
            p3 = k.ps()
            k.proj(p3, wg2, c * 128, 128, h, T)
            k.act(abs_, p3, AF.Copy)
            k.tt(ya.c(c), pcv, abs_, ALU.mult)
            k.copy(u_ext.cs(c, _sl(0, 2)), u_ext.cs(c, _sl(T, T + 2)), eng="dve")
        for c in range(2):
            k.copy(qtb.c(c), qtil.c(c))
            p = k.ps()
            k.proj(p, wg3, 256 + c * 128, 128, h, T)
            k.act(sr, p, AF.Silu)
            for hh_ in range(2):
                hr = hh_ * 64
                pc_ = k.ps()
                k.mm(pc_[hr:hr + 64, :], Sib[hr:hr + 64, c, hr:hr + 64], qtb.cs(c, (slice(hr, hr + 64), slice(None))),
                     start=True, stop=True, tp=(hr, hr))
                k.tt(o32[hr:hr + 64, :], pc_[hr:hr + 64, :], opart.cs(c, (slice(hr, hr + 64), slice(None))), ALU.add)
            k.act(sqb, o32, AF.Square)
            pss = k.ps()
            k.mm(pss, G.onesbd, sqb)
            k.act(rs, pss, AF.Sqrt, bias=EPS, scale=1.0 / 64.0)
            k.recip(rs, rs)
            k.stt(t1, o32, cv[:, C_GNG + c:C_GNG + c + 1], rs, ALU.mult, ALU.mult)
            k.tt(yb.c(c), t1, sr, ALU.mult)
        hloc, pcum = ld1, ld2
        k.dma(ld1, Tl(G.s_hloc.ap[:, :, t0:t0 + T], (G.sb_hl[it],)), sem="ld_1")
        k.dma(ld2, Tl(G.s_pcum.ap[:, :, t0:t0 + T], (G.sb_pc[it],)), sem="ld_2")
        wg4 = k.wload(G.w_in, KC, 512, 0, WI_SU)
        for c in range(2):
            p = k.ps()
            k.proj(p, wg4, c * 128, 128, h, T)
            k.act(sus.c(c), p, AF.Copy)
        pmix = [k.ps(hold=True), k.ps(hold=True)]
        for b in range(4):
            p = k.ps()
            for kc in range(KC):
                k.mm(p[:, 0:256], h.cs(kc, _sl(b * 128, (b + 1) * 128)), wg4[:, kc, 256:512], start=(kc == 0), stop=(kc == KC - 1))
            k.memset(st4, 0.0, eng="dve")
            k.act(sv32, p[:, 0:256], AF.Identity, accum=st4[:, 0:1])
            k.ts(st4[:, 1:2], st4[:, 0:1], -1.0 / 256.0, None, ALU.mult)
            k.act(junk, sv32, AF.Square, bias=st4[:, 1:2], accum=st4[:, 2:3])
            k.act(st4[:, 3:4], st4[:, 2:3], AF.Sqrt, bias=EPS, scale=1.0 / 256.0)
            k.recip(st4[:, 3:4], st4[:, 3:4])
            k.ts(vn, sv32, st4[:, 1:2], st4[:, 3:4], ALU.add, ALU.mult)
            k.tt(vn, vn, lnG, ALU.mult)
            vb = vnb[b % 2]
            k.tt(vb, vn, lnB, ALU.add)
            for g in range(4):
                c, hr = g // 2, (g % 2) * 64
                k.mm(pmix[c][hr:hr + 64, b * 128:(b + 1) * 128], vb[:, g * 64:(g + 1) * 64], wm[:, g, :], start=True, stop=True, tp=(0, hr))
        for c in range(2):
            k.tt(Tl(m1.ap.rearrange("p (b i) -> p b i", i=128), m1.bufs),
                 Tl(pmix[c].ap.rearrange("p (b i) -> p b i", i=128), pmix[c].bufs),
                 Tl(bsT.ap[:, c, :].unsqueeze(1).to_broadcast([128, 4, 128]), bsT.bufs), ALU.add)
            k.tt(yc.c(c), sus.c(c), m1, ALU.mult)
            k.release(pmix[c])
        wg5 = k.wload(G.w_in, KC, 256, 0, WI_RG)
        for c in range(2):
            p = k.ps()
            k.proj(p, wg5, c * 128, 128, h, T)
            k.act(gg, p, AF.Gelu)
            k.stt(hf, pcum.c(c), hi[:, c:c + 1], hloc.c(c), ALU.mult, ALU.add)
            k.tt(yd.c(c), hf, gg, ALU.mult)
        ys = [ya, yb, yc, yd]
        pend = None
        pri = 0

        def flush_pend(pd):
            mg_, pr_, kb_, m_ = pd
            k.mm(mg_, k.ident, pr_, start=(kb_ == 0), stop=(kb_ == 3))
            if kb_ == 3:
                k.act(merged.c(m_), mg_, AF.Copy)
                k.release(mg_)

        for m in range(KC):
            wbr = k.wload(G.w_br, 8, 128, 0, m * 128)
            wgt = k.wload(G.w_gate, KC, 512, 0, m * 512)
            mg = k.ps(hold=True)
            for kb in range(4):
                pb = k.ps()
                for kc in range(2):
                    k.mm(pb, wbr[:, kb * 2 + kc, 0:128], ys[kb].c(kc), start=(kc == 0), stop=(kc == 1))
                pg = k.ps()
                k.proj(pg, wgt, kb * 128, 128, h, T)
                if pend is not None:
                    flush_pend(pend)
                gt = gts[kb % 2]
                pr = prods[pri % 4]
                pri += 1
                k.act(gt, pg, AF.Sigmoid, bias=cv[:, C_BG + kb * 8 + m:C_BG + kb * 8 + m + 1])
                k.tt(pr, pb, gt, ALU.mult)
                pend = (mg, pr, kb, m)
                if not PIPE['merge']:
                    flush_pend(pend)
                    pend = None
        if pend is not None:
            flush_pend(pend)
        wmx = None
        for m in range(KC):
            if m % 4 == 0:
                wmx = k.wload(G.w_mix, KC, 512, 0, (m // 4) * 512)
            p = k.ps()
            k.proj(p, wmx, (m % 4) * 128, 128, merged, T)
            k.act(ybuf7.c(m), p, AF.Copy)
        k.postnorm_residual(xs, ybuf7, C_G + 1 * 8, cv, sq, rstd, tmp)
        k.rmsnorm_h(h, xs, T, C_G + 2 * 8, cv, sq, rstd)
        wqs = None
        for m in range(KC):
            if m % 4 == 0:
                wqs = k.wload(G.w_q, KC, 512, 0, (m // 4) * 512)
            p = k.ps()
            k.proj(p, wqs, (m % 4) * 128, 128, h, T)
            k.act(qT.c(m), p, AF.Copy)
        def scores(hd_):
            E_ = Es[hd_ % 2]
            for mc in range(2):
                p = k.ps()
                for dc in range(2):
                    k.mm(p, KT.cs(2 * hd_ + dc, _sl(mc * 128, (mc + 1) * 128)), qT.c(2 * hd_ + dc), start=(dc == 0), stop=(dc == 1))
                k.act(E_.c(mc), p, AF.Exp, scale=1.0 / 16.0)
        if PIPE['xa']:
            scores(0)
        for hd in range(4):
            if PIPE['xa']:
                if hd + 1 < 4:
                    scores(hd + 1)
            else:
                scores(hd)
            E = Es[hd % 2]
            pden = k.ps()
            for mc in range(2):
                k.mm(pden, k.ones, E.c(mc), start=(mc == 0), stop=(mc == 1))
            k.recip(rden, pden)
            for dc in range(2):
                po = k.ps()
                for mc in range(2):
                    k.mm(po, Vt.cs(mc, _sl((2 * hd + dc) * 128, (2 * hd + dc + 1) * 128)), E.c(mc), start=(mc == 0), stop=(mc == 1))
                k.tt(oT.c(2 * hd + dc), po, rden, ALU.mult)
        wos = None
        for m in range(KC):
            if m % 4 == 0:
                wos = k.wload(G.w_o, KC, 512, 0, (m // 4) * 512)
            p = k.ps()
            k.proj(p, wos, (m % 4) * 128, 128, oT, T)
            k.act(ybuf8.c(m), p, AF.Copy)
        k.postnorm_residual(xs, ybuf8, C_G + 3 * 8, cv, sq, rstd, tmp)
    k.S.flush()
    st0.close()
    k.cur = None


def fused_phase_C(k, G, l, NT):
    cv = G.cv
    st = ExitStack()
    k.cur = st
    k.lyr = l
    k.S.barrier()
    k.init_ring(8)
    R = k.sb([128, 2 * KC * T], BF16)
    rb = [Buf() for _ in range(2 * KC)]
    h = Tl(R.ap[:, 0:KC * T].rearrange("p (a b) -> p a b", b=T), rb[0:KC], rb[0:KC])
    ybuf = Tl(R.ap.bitcast(F32).rearrange("p (a b) -> p a b", b=T), rb)
    ybuf.cb = [(rb[2 * m], rb[2 * m + 1]) for m in range(KC)]
    sq = k.sb([128, KC, T], BF16, nchunk=KC)
    rstd = k.sb([128, T], F32)
    tmp = k.sb([128, T], F32)
    hh = k.sb([128, KC, 4], BF16, nchunk=KC)
    hal = [k.sb([128, NFC, 2], BF16) for _ in range(2)]
    us = [k.sb([128, T], BF16) for _ in range(3)]
    gls = [k.sb([128, T], F32) for _ in range(2)]
    hmid = k.sb([128, 22, T], BF16, nchunk=22)
    k.rmsnorm_h(hh, [G.xhs.c(kc) for kc in range(KC)], 4, C_G + 5 * 8, cv, sq, rstd)
    ui = 0
    cpend = None

    def conv(ch, u, dff, j, cur):
        pc_ = k.ps()
        d0, d1, d2 = dff[:, j * 3 + 0, :], dff[:, j * 3 + 1, :], dff[:, j * 3 + 2, :]
        k.mm(pc_[:, 0:T], d2, u[:, 0:T], start=True, stop=False)
        k.mm(pc_[:, 1:T], d1, u[:, 0:T - 1], start=False, stop=False)
        k.mm(pc_[:, 0:1], d1, cur[:, ch, 1:2], start=False, stop=False)
        k.mm(pc_[:, 2:T], d0, u[:, 0:T - 2], start=False, stop=False)
        k.mm(pc_[:, 0:2], d0, cur[:, ch, 0:2], start=False, stop=True)
        fcb = cv[:, C_FCB + ch:C_FCB + ch + 1]
        if j % 2 == 0:
            k.act(gls[(ch // 2) % 2], pc_, AF.Gelu, bias=fcb)
        else:
            k.stt(hmid.c(ch // 2), pc_, fcb, gls[(ch // 2) % 2], ALU.add, ALU.mult)

    for it in range(NT):
        xs = G.xs[it]
        cur, nxt = hal[it % 2], hal[(it + 1) % 2]
        k.rmsnorm_h(h, xs, T, C_G + 5 * 8, cv, sq, rstd)
        for g in range(11):
            wup = k.wload(G.w_up, KC, 512, 0, g * 512, cache=("up", l, g))
            dff = k.wload(G.dffd, 12, 128, g * 12, 0, cache=("dff", l, g))
            for j in range(4):
                ch = g * 4 + j
                if it == 0:
                    ph = k.ps()
                    k.proj(ph, wup, j * 128, 128, hh, 4)
                    k.act(cur[:, ch, :], ph[:, 2:4], AF.Copy)
                p = k.ps()
                k.proj(p, wup, j * 128, 128, h, T)
                if cpend is not None:
                    conv(*cpend)
                u = us[ui % 3]
                ui += 1
                k.act(u, p, AF.Copy)
                k.act(nxt[:, ch, :], p[:, T - 2:T], AF.Copy)
                cpend = (ch, u, dff, j, cur)
                if not PIPE['conv']:
                    conv(*cpend)
                    cpend = None
        if cpend is not None:
            conv(*cpend)
        cpend = None
        for cg in range(2):
            accs = [k.ps(hold=True) for _ in range(4)]
            for (r0, rows) in ((0, 8), (8, 8), (16, 6)):
                wd = k.wload(G.w_dn, rows, 512, r0, cg * 512, cache=("dn", l, r0, cg))
                for mi in range(4):
                    for r in range(rows):
                        k.mm(accs[mi], wd[:, r, mi * 128:(mi + 1) * 128], hmid.c(r0 + r), start=(r0 + r == 0), stop=(r0 + r == 21))
            for mi in range(4):
                k.act(ybuf.c(cg * 4 + mi), accs[mi], AF.Copy)
                k.release(accs[mi])
        k.postnorm_residual(xs, ybuf, C_G + 6 * 8, cv, sq, rstd, tmp)
    k.S.flush()
    st.close()
    k.cur = None


def build_fused(NT, L=2):
    NTOK = NT * T
    nc = bass.Bass("TRN2", target_bir_lowering=False)
    with ExitStack() as st:
        k = KB(nc, st)
        G = NS()
        xT = k.dram_in("xT", [128, KC, NTOK])
        xh = k.dram_in("xh", [128, KC, 4])
        G.memT = k.dram_in("memT", [128, KC, 256])
        seld = k.dram_in("sel", [128, 24])
        selpd = k.dram_in("selp", [128, 8])
        cvd = k.dram_in("cvec", [L, 128, NV])
        G.w_in = k.dram_in("w_in", [L, 128, KC, 2832])
        G.dlru = k.dram_in("diag_lru", [L, 128, 8, 128])
        G.bda = k.dram_in("bd_a", [L, 128, 2, 128])
        G.bdx = k.dram_in("bd_x", [L, 128, 2, 128])
        G.wal = k.dram_in("walpha", [L, 16, 256])
        G.bal = k.dram_in("balpha", [L, 1, 256])
        G.w_gate = k.dram_in("w_gate", [L, 128, KC, 4096])
        G.w_br = k.dram_in("w_br", [L, 128, 8, 1024])
        G.w_mix = k.dram_in("w_mix", [L, 128, KC, 1024])
        G.w_q = k.dram_in("w_q", [L, 128, KC, 1024])
        G.w_kv = k.dram_in("w_kv", [L, 128, KC, 2048])
        G.w_o = k.dram_in("w_o", [L, 128, KC, 1024])
        G.dsc = k.dram_in("diag_sc", [L, 128, 6, 128])
        G.wmd = k.dram_in("WmT", [L, 128, 4, 128])
        G.lnG = k.dram_in("lnG", [L, 128, 256])
        G.lnB = k.dram_in("lnB", [L, 128, 256])
        G.bsT = k.dram_in("bsT", [L, 128, 2, 128])
        G.w_up = k.dram_in("w_up", [L, 128, KC, 2 * DFF])
        G.dffd = k.dram_in("diag_ffn", [L, 128, NFC * 3, 128])
        G.w_dn = k.dram_in("w_dn", [L, 128, 22, 1024])
        xo = k.dram_out("xo", [128, KC, NTOK])
        scr = lambda nm, shp: Tl(nc.dram_tensor(nm, shp, F32).ap(), [Buf()])
        G.s_opart = scr("s_opart", [128, 2, NTOK]); G.sb_op = [Buf() for _ in range(NT)]
        G.s_qtil = scr("s_qtil", [128, 2, NTOK]); G.sb_qt = [Buf() for _ in range(NT)]
        G.s_hloc = scr("s_hloc", [128, 2, NTOK]); G.sb_hl = [Buf() for _ in range(NT)]
        G.s_pcum = scr("s_pcum", [128, 2, NTOK]); G.sb_pc = [Buf() for _ in range(NT)]
        G.link_src = scr("link_src", [128, NLINK])
        G.link_all = scr("link_all", [8 * 128, NLINK])
        G.halo_src = scr("halo_src", [128, 32])
        G.halo_all = scr("halo_all", [8 * 128, 32])

        k.init_psum()
        k.consts()
        G.cv = k.sb([128, NV], F32)
        G.sel = k.sb([128, 24], F32); k.dma(G.sel, seld, sem="c0")
        G.selp = k.sb([128, 8], F32); k.dma(G.selp, selpd, sem="c1")
        G.xhs = k.sb([128, KC, 4], F32, nchunk=KC)
        k.dma(Tl(G.xhs.ap, G.xhs.cb), xh, sem="c2")
        G.trib = k.sb([128, 128], F32)
        k.memset(G.trib, 1.0)
        k.aselect(G.trib, [[1, 128]], ALU.is_ge, 0, -1)
        G.triu = k.sb([128, 128], F32)
        k.copy(G.triu, G.trib)
        k.aselect(G.trib[:, 64:128], [[0, 64]], ALU.is_ge, -64, 1)
        G.tribb = k.sb([128, 128], BF16)
        k.copy(G.tribb, G.trib)
        mask4 = k.sb([128, 4, 128], BF16)
        for b in range(4):
            k.copy(mask4[:, b, :], G.trib)
        G.mask4f = Tl(mask4.ap.rearrange("p b i -> p (b i)"), mask4.bufs)
        G.onesbd = k.sb([128, 128], BF16)
        k.memset(G.onesbd, 1.0)
        k.aselect(G.onesbd[:, 0:64], [[0, 64]], ALU.is_ge, 63, -1)
        k.aselect(G.onesbd[:, 64:128], [[0, 64]], ALU.is_ge, -64, 1)
        x = k.sb([128, KC, NTOK], F32)
        xb = [[Buf() for _ in range(KC)] for _ in range(NT)]
        G.xs = [[Tl(x.ap[:, kc, it * T:(it + 1) * T], (xb[it][kc],)) for kc in range(KC)] for it in range(NT)]
        for it in range(NT):
            k.dma(Tl(x.ap[:, :, it * T:(it + 1) * T], xb[it]), xT[:, :, it * T:(it + 1) * T], sem="xld%d" % it)
        k.S.flush()
        for l in range(L):
            k.S.barrier()
            k.dma(G.cv, Tl(cvd.ap[l], cvd.bufs), sem="cv")
            fused_phase_A(k, G, l, NT)
            fused_phase_B(k, G, l, NT)
            fused_halo_exchange(k, G, NT)
            fused_phase_C(k, G, l, NT)
            if l < L - 1:
                fused_halo_exchange(k, G, NT)
        k.S.barrier()
        for it in range(NT):
            k.dma(Tl(xo.ap[:, :, it * T:(it + 1) * T], xo.bufs), Tl(x.ap[:, :, it * T:(it + 1) * T], xb[it]), sem="xst%d" % it)
        k.S.wait_all("sp", list(xo.bufs))
        k.S.flush()
    return nc


_FUSED = {}


def kernel(**inputs):
    x = np.asarray(inputs["x"], dtype=np.float32)
    mem = np.asarray(inputs["mem"], dtype=np.float32)
    B_, S_, _ = x.shape
    NCORE = 8
    NC_PER = NCORE // B_
    NTOK = S_ // NC_PER
    NT = NTOK // T
    L = np.asarray(inputs["norm_g"]).shape[0]
    cores = list(range(NCORE))
    xc = [_tok_fm(x[c // NC_PER, (c % NC_PER) * NTOK:(c % NC_PER + 1) * NTOK]) for c in cores]
    xh = _halo(xc, NC_PER)
    memT = [_tok_fm(mem[b]) for b in range(B_)]
    Ps = [prep_layer(inputs, l) for l in range(L)]
    W = {k_: np.ascontiguousarray(np.stack([P[k_] for P in Ps], axis=0)) for k_ in Ps[0]}
    in_maps = []
    for c in cores:
        r, gb = c % NC_PER, (c // NC_PER) * NC_PER
        s = np.zeros((3, 8), np.float32)
        for j in range(3):
            src = r - 3 + j
            if src >= 0:
                s[j, gb + src] = 1.0
        sp = np.zeros((8,), np.float32)
        if r > 0:
            sp[c - 1] = 1.0
        m = dict(W)
        m.update(xT=xc[c], xh=xh[c], memT=memT[c // NC_PER],
                 sel=np.ascontiguousarray(np.broadcast_to(s.reshape(1, 24), (128, 24))),
                 selp=np.ascontiguousarray(np.broadcast_to(sp.reshape(1, 8), (128, 8))))
        in_maps.append(m)
    key = (NT, L)
    if key not in _FUSED:
        _FUSED[key] = build_fused(NT, L)
    res = run_bass_kernel_spmd(_FUSED[key], in_maps, core_ids=cores).results
    out = np.zeros((B_, S_, D), np.float32)
    for c in cores:
        out[c // NC_PER, (c % NC_PER) * NTOK:(c % NC_PER + 1) * NTOK] = np.asarray(res[c]["xo"]).transpose(2, 1, 0).reshape(NTOK, D)
    return out
```
